# Optimizing a Trainium2 kernel written in Bass

```python
import jax, jax.numpy as jnp
from jax import lax
import numpy as np

D_MODEL = 2048
BATCH = 2
SEQ = 4096
DEPTH = 1

CHUNK = 64
D_MIX = D_MODEL
D_ATTN = D_MIX // 2
D_POOL = D_MIX - D_ATTN
HEAD_DIM = 128
N_HEADS = D_ATTN // HEAD_DIM
POOL_WINDOWS = (2, 4, 8, 16)
N_POOL_GROUPS = len(POOL_WINDOWS)
POOL_GROUP_DIM = D_POOL // N_POOL_GROUPS
D_FF = ((8 * D_MODEL // 3 + 127) // 128) * 128
Q_BLOCK = 128
N_MOD = 9
D_IN_PROJ = 3 * D_ATTN + N_HEADS + D_POOL
EPS = 1e-6

kernel_name = "hybrid_fox_pool_macaron_block"


def _rmsnorm(x, g):
    xf = x.astype(jnp.float32)
    xf = xf * lax.rsqrt(jnp.mean(xf * xf, axis=-1, keepdims=True) + EPS)
    return xf.astype(x.dtype) * g


def _modulate(h, shift, scale):
    return h * (1.0 + scale[:, None, :]) + shift[:, None, :]


def _swiglu(h, w_in, w_out):
    a, b = jnp.split(h @ w_in, 2, axis=-1)
    return (jax.nn.silu(a) * b) @ w_out


def _forgetting_attention(q, k, v, log_f):
    S = q.shape[2]
    scale = HEAD_DIM ** -0.5
    F = jnp.cumsum(log_f, axis=-1)
    outs = []
    for i in range(S // Q_BLOCK):
        qs, qe = i * Q_BLOCK, (i + 1) * Q_BLOCK
        qb = q[:, :, qs:qe]
        kb = k[:, :, :qe]
        vb = v[:, :, :qe]
        logits = jnp.einsum('bhqd,bhkd->bhqk', qb, kb).astype(jnp.float32) * scale
        logits = logits + (F[:, :, qs:qe, None] - F[:, :, None, :qe])
        causal = (qs + jnp.arange(Q_BLOCK))[:, None] >= jnp.arange(qe)[None, :]
        logits = jnp.where(causal[None, None], logits, -jnp.inf)
        p = jax.nn.softmax(logits, axis=-1)
        outs.append(jnp.einsum('bhqk,bhkd->bhqd', p.astype(vb.dtype), vb))
    return jnp.concatenate(outs, axis=2)


def _multiscale_pool(u, pool_w, pool_scale):
    B, S, _ = u.shape
    ug = u.reshape(B, S, N_POOL_GROUPS, POOL_GROUP_DIM)
    pos = jnp.arange(S)
    pooled = []
    for g, w in enumerate(POOL_WINDOWS):
        xg = ug[:, :, g].astype(jnp.float32)
        cs0 = jnp.pad(jnp.cumsum(xg, axis=1), ((0, 0), (1, 0), (0, 0)))
        lag = jnp.pad(cs0, ((0, 0), (w - 1, 0), (0, 0)))[:, :S]
        count = jnp.minimum(pos + 1, w).astype(jnp.float32)[None, :, None]
        mean = (cs0[:, 1:] - lag) / count
        pooled.append((mean - xg).astype(u.dtype))
    p = jnp.stack(pooled, axis=2)
    p = jnp.einsum('bsgc,gcd->bsgd', p, pool_w)
    return p.reshape(B, S, D_POOL) * pool_scale


def _hybrid_mixer(h, w_in, b_forget, q_norm_g, k_norm_g, pool_w, pool_scale, w_out):
    B, S, _ = h.shape
    proj = h @ w_in
    q, k, v, f_logit, u = jnp.split(
        proj, [D_ATTN, 2 * D_ATTN, 3 * D_ATTN, 3 * D_ATTN + N_HEADS], axis=-1)

    def heads(t):
        return t.reshape(B, S, N_HEADS, HEAD_DIM).transpose(0, 2, 1, 3)

    q = _rmsnorm(heads(q), q_norm_g)
    k = _rmsnorm(heads(k), k_norm_g)
    v = heads(v)
    log_f = jax.nn.log_sigmoid((f_logit + b_forget).astype(jnp.float32)).transpose(0, 2, 1)
    attn = _forgetting_attention(q, k, v, log_f)
    attn = attn.transpose(0, 2, 1, 3).reshape(B, S, D_ATTN)
    pool = _multiscale_pool(u, pool_w, pool_scale)
    return jnp.concatenate([attn, pool], axis=-1) @ w_out


def _nrm(k, shape, scale):
    return jax.random.normal(k, shape, jnp.float32) * scale


def setup_inputs(seed: int = 0) -> dict:
    key = jax.random.key(seed)
    ks = jax.random.split(key, 20)
    L = DEPTH
    return {
        "x": _nrm(ks[0], (BATCH, SEQ, D_MODEL), 1.0),
        "c": _nrm(ks[1], (BATCH, D_MODEL), 1.0),
        "w_ada": _nrm(ks[2], (L, D_MODEL, N_MOD * D_MODEL), 0.5 * D_MODEL ** -0.5),
        "b_ada": _nrm(ks[3], (L, N_MOD * D_MODEL), 0.02),
        "ffn1_norm_g": 1.0 + _nrm(ks[4], (L, D_MODEL), 0.05),
        "ffn1_w_in": _nrm(ks[5], (L, D_MODEL, 2 * D_FF), D_MODEL ** -0.5),
        "ffn1_w_out": _nrm(ks[6], (L, D_FF, D_MODEL), D_FF ** -0.5),
        "mix_norm_g": 1.0 + _nrm(ks[7], (L, D_MODEL), 0.05),
        "w_in": _nrm(ks[8], (L, D_MODEL, D_IN_PROJ), D_MODEL ** -0.5),
        "b_forget": jax.random.uniform(ks[9], (L, N_HEADS), jnp.float32, 1.0, 4.0),
        "q_norm_g": 1.0 + _nrm(ks[10], (L, HEAD_DIM), 0.05),
        "k_norm_g": 1.0 + _nrm(ks[11], (L, HEAD_DIM), 0.05),
        "pool_w": _nrm(ks[12], (L, N_POOL_GROUPS, POOL_GROUP_DIM, POOL_GROUP_DIM), POOL_GROUP_DIM ** -0.5),
        "pool_scale": 1.0 + _nrm(ks[13], (L, D_POOL), 0.1),
        "w_out": _nrm(ks[14], (L, D_MIX, D_MODEL), D_MIX ** -0.5),
        "ffn2_norm_g": 1.0 + _nrm(ks[15], (L, D_MODEL), 0.05),
        "ffn2_w_in": _nrm(ks[16], (L, D_MODEL, 2 * D_FF), D_MODEL ** -0.5),
        "ffn2_w_out": _nrm(ks[17], (L, D_FF, D_MODEL), D_FF ** -0.5),
        "final_norm_g": 1.0 + _nrm(ks[18], (D_MODEL,), 0.05),
    }


def reference(x, c, w_ada, b_ada, ffn1_norm_g, ffn1_w_in, ffn1_w_out, mix_norm_g,
              w_in, b_forget, q_norm_g, k_norm_g, pool_w, pool_scale, w_out,
              ffn2_norm_g, ffn2_w_in, ffn2_w_out, final_norm_g):
    c_act = jax.nn.silu(c)
    for l in range(DEPTH):
        mod = c_act @ w_ada[l] + b_ada[l]
        sh1, sc1, g1, sh2, sc2, g2, sh3, sc3, g3 = jnp.split(mod, N_MOD, axis=-1)
        h = _modulate(_rmsnorm(x, ffn1_norm_g[l]), sh1, sc1)
        x = x + 0.5 * g1[:, None, :] * _swiglu(h, ffn1_w_in[l], ffn1_w_out[l])
        h = _modulate(_rmsnorm(x, mix_norm_g[l]), sh2, sc2)
        x = x + g2[:, None, :] * _hybrid_mixer(h, w_in[l], b_forget[l], q_norm_g[l], k_norm_g[l],
                                              pool_w[l], pool_scale[l], w_out[l])
        h = _modulate(_rmsnorm(x, ffn2_norm_g[l]), sh3, sc3)
        x = x + 0.5 * g3[:, None, :] * _swiglu(h, ffn2_w_in[l], ffn2_w_out[l])
    return _rmsnorm(x, final_norm_g)
```

```python
import numpy as np
import concourse.bass as bass
import concourse.mybir as mybir
from concourse.bass_utils import run_bass_kernel_spmd
from contextlib import ExitStack

F32 = mybir.dt.float32
BF16 = mybir.dt.bfloat16
AF = mybir.ActivationFunctionType
ALU = mybir.AluOpType

ENGS = ["tensor", "vector", "scalar", "gpsimd", "sync"]
D = 2048
T = 1024
NCH = 16
DFF = 5504
NF = 43
GROUPS = [(0, 11), (11, 11), (22, 11), (33, 10)]
NSLOT = 4
EPS = 1e-6
DEBUG = False
STOP = 99


class _StopBody(Exception):
    pass


class Prog:
    def __init__(self, nc, es):
        self.nc = nc
        self.es = es
        self.dry = False
        self.sems = {}
        self.reset()

    def reset(self):
        self.q = {e: [] for e in ENGS}
        self.cnt = {k: 0 for k in self.sems}
        self.waited = {}
        self.last_w = {}
        self.readers = {}

    def new_sem(self, key):
        if key not in self.sems:
            self.sems[key] = self.es.enter_context(self.nc.semaphore(key))
        self.cnt[key] = 0
        return key

    def op(self, eng, fn, reads=(), writes=(), acc=(), deps=(), sem=None):
        if self.dry:
            return None
        mykey = "E_" + eng
        alld = list(deps)
        for r in reads:
            if r in self.last_w:
                alld.append(self.last_w[r])
        for w in writes:
            if w in self.last_w:
                alld.append(self.last_w[w])
            alld.extend(self.readers.get(w, ()))
        for w in acc:
            if w in self.last_w and self.last_w[w][0] != mykey:
                alld.append(self.last_w[w])
            alld.extend(self.readers.get(w, ()))
        need = {}
        for (k, v) in alld:
            need[k] = max(need.get(k, 0), v)
        waits = []
        for k, v in need.items():
            if v > self.waited.get((eng, k), 0):
                self.waited[(eng, k)] = v
                waits.append((k, v))
        if sem is None:
            key, inc = mykey, 1
        else:
            key, inc = sem, 16
        self.cnt[key] += inc
        tok = (key, self.cnt[key])
        self.q[eng].append((waits, fn, key, inc))
        for r in reads:
            self.readers.setdefault(r, []).append(tok)
        for w in list(writes) + list(acc):
            self.last_w[w] = tok
            self.readers[w] = []
        return tok

    def join(self, sem, bufs):
        if self.dry:
            return
        for b in bufs:
            self.last_w[b] = (sem, self.cnt[sem])

    def retire(self, names):
        if self.dry:
            return []
        toks = []
        for n in names:
            if n in self.last_w:
                toks.append(self.last_w.pop(n))
            toks.extend(self.readers.pop(n, ()))
        need = {}
        for (k, v) in toks:
            need[k] = max(need.get(k, 0), v)
        return list(need.items())

    def seed(self, names, toks):
        if self.dry:
            return
        for n in names:
            self.readers.setdefault(n, []).extend(toks)

    def emit(self, final_waits=()):
        nc = self.nc
        with nc.Block() as block:
            for e in ENGS:
                items = self.q[e]
                fw = list(final_waits) if e == "sync" else []

                def body(eh, items=items, fw=fw):
                    for (waits, fn, key, inc) in items:
                        for (k, v) in waits:
                            eh.wait_ge(self.sems[k], v)
                        inst = fn(eh)
                        inst.then_inc(self.sems[key], inc)
                    for (k, v) in fw:
                        eh.wait_ge(self.sems[k], v)

                getattr(block, e)(body)


class WStream:
    def __init__(self, P, slots):
        self.P = P
        self.slots = slots
        self.plan = []
        self.idx = 0
        self.issued = 0

    def start_real(self):
        self.idx = 0
        self.issued = 0

    def get(self, src, n):
        P = self.P
        if P.dry:
            self.plan.append((src, n))
            return self.slots[0], "W0"
        while self.issued < min(len(self.plan), self.idx + NSLOT):
            k = self.issued
            s = k % NSLOT
            sr, nn = self.plan[k]
            P.op("gpsimd", lambda e, s=s, sr=sr, nn=nn: e.dma_start(out=self.slots[s][:, 0:nn], in_=sr),
                 writes=[f"W{s}"], sem=f"dW{s}")
            self.issued += 1
        s = self.idx % NSLOT
        self.idx += 1
        return self.slots[s], f"W{s}"


def build_nc():
    nc = bass.Bass("TRN2", target_bir_lowering=False)

    def din(name, shape, dt=F32):
        return nc.dram_tensor(name, list(shape), dt, kind="ExternalInput").ap()

    xT_d = din("xT", [128, NCH, T])
    cT_d = din("cT", [128, 16])
    wada_d = din("wada", [144, 128, 2048])
    bada_d = din("badaT", [128, 144])
    gains_d = din("gainsT", [128, 64])
    w1in_d = din("w1in", [86, 128, 2048])
    w1out_d = din("w1out", [64, 128, 1408])
    w2in_d = din("w2in", [86, 128, 2048])
    w2out_d = din("w2out", [64, 128, 1408])
    wqkvu_d = din("wqkvu", [32, 128, 2048])
    wf_d = din("wf", [128, 128])
    poolw_d = din("poolw", [4, 128, 512])
    womix_d = din("womix", [16, 128, 2048])
    small_d = din("small", [128, 16])
    wsel_d = din("wsel", [128, 4])
    wown_d = din("wsel_own", [128, 4])
    maskc_d = din("maskc", [128, 512])
    invc_d = din("invc", [128, 512])
    ident_d = din("ident8", [128, 8])
    outT_d = nc.dram_tensor("outT", [128, NCH, T], F32, kind="ExternalOutput").ap()
    dbg_d = None
    if DEBUG:
        dbg_d = nc.dram_tensor("dbg", [4, 128, NCH, T], F32, kind="ExternalOutput").ap()

    agf_in = nc.dram_tensor("agf_in", [136, 1024], F32)
    agf_out = nc.dram_tensor("agf_out", [544, 1024], F32)
    agkv_in = [nc.dram_tensor(f"agkv_in{q}", [32, 16384], BF16) for q in range(4)]
    agkv_out = [nc.dram_tensor(f"agkv_out{q}", [128, 16384], BF16) for q in range(4)]

    with ExitStack() as es:
        P = Prog(nc, es)
        for e in ENGS:
            P.new_sem("E_" + e)
        NW = 52200
        ar = es.enter_context(nc.sbuf_tensor("arena", [128, NW], F32))
        psb = [es.enter_context(nc.psum_tensor(f"psb{i}", [128, 512], F32)) for i in range(8)]

        def fv(off, n):
            return ar[:, off:off + n]

        def bv(off, nw):
            return ar[:, off:off + nw].bitcast(BF16)

        OW = 16384
        OR2 = OW + NSLOT * 1024
        OR3 = OR2 + 8192
        OR4 = OR3 + 8192
        OR5 = OR4 + 4096
        OM = OR5 + 9216
        XT = fv(0, 16384).rearrange("p (c t) -> p c t", t=T)
        wslots = [bv(OW + s * 1024, 1024) for s in range(NSLOT)]
        hT = bv(OR2, 8192).rearrange("p (c t) -> p c t", t=T)
        TL = fv(OR2, 4096).rearrange("p (r x s) -> p r x s", r=4, s=16)
        T1 = fv(OR2 + 4096, 1152).rearrange("p (m s) -> p m s", s=144)
        T2 = fv(OR2 + 5248, 1152).rearrange("p (m s) -> p m s", s=144)
        pooled = bv(OR2 + 6400, 1024).rearrange("p (c t) -> p c t", t=T)
        ptmp = fv(OR2 + 7424, 128)
        Ksl = [bv(OR2 + i * 1024, 1024).rearrange("p (r t) -> p r t", r=4) for i in range(2)]
        Vsl = [bv(OR2 + 2048 + i * 1024, 1024).rearrange("p (r m d) -> p r m d", r=4, m=4) for i in range(2)]
        Fown = fv(OR2 + 4096, 1024)
        negFT = fv(OR2 + 5120, 256)
        Fbc = [fv(OR2 + 5376 + i * 1024, 1024) for i in range(2)]
        gT = bv(OR3, 5632).rearrange("p (c t) -> p c t", t=T)
        sa = [fv(OR3 + 5632 + i * 512, 512) for i in range(2)]
        sqk = [bv(OR3 + i * 256, 256) for i in range(2)]
        rk = [fv(OR3 + 512 + i * 512, 512) for i in range(2)]
        mixT = bv(OR3, 8192).rearrange("p (c t) -> p c t", t=T)
        sq = [bv(OR4 + i * 256, 256) for i in range(2)]
        rstd = fv(OR4 + 512, 1024)
        tt = [fv(OR4 + 1536 + i * 512, 512) for i in range(2)]
        lnv = fv(OR4 + 2560, 512)
        kst = [bv(OR4 + i * 1024, 1024).rearrange("p (h t) -> p h t", h=2) for i in range(2)]
        vst = [bv(OR4 + i * 1024, 256).rearrange("p (m d) -> p m d", m=4) for i in range(2)]
        LFown = fv(OR4 + 2048, 1024)
        qnT = bv(OR4, 4096).rearrange("p (c t) -> p c t", t=T)
        uext = fv(OR5, 9216).rearrange("p (x s) -> p x s", s=144)
        LF = fv(OR5, 4096)
        Fg = fv(OR5 + 4096, 4096)
        ones1k = fv(OR5 + 8192, 1024)
        Sb = [fv(OR5 + i * 512, 512) for i in range(3)]
        PT = [bv(OR5 + 1536 + i * 256, 256) for i in range(4)]
        OLs = [fv(OR5 + 2560 + i * 512, 512) for i in range(2)]
        rl = fv(OR5 + 3584, 512)
        Fm = fv(OR5 + 4096, 1024)
        o = [OM]

        def misc(n):
            a = o[0]
            o[0] += n
            return a
        modT = fv(misc(144), 144)
        bada = fv(misc(144), 144)
        der = fv(misc(144), 144)
        gains = fv(misc(64), 64)
        cT = fv(misc(16), 16)
        cact = bv(misc(8), 8)
        small = fv(misc(16), 16)
        smd = fv(misc(8), 8)
        wsel = fv(misc(4), 4)
        wown = fv(misc(4), 4)
        ident8 = fv(misc(8), 8)
        ones_bf = bv(misc(64), 64)
        ones8 = fv(misc(128), 128)
        maskc = fv(misc(512), 512).rearrange("p (r t) -> p r t", r=4)
        invc = fv(misc(512), 512).rearrange("p (g t) -> p g t", g=4)
        wf = bv(misc(64), 64)
        assert o[0] <= NW, o[0]

        for k in ["dW%d" % s for s in range(NSLOT)] + ["dx0", "dx1", "dx2", "dx3", "dconst", "dst0", "dst1",
                                                       "dagf", "dtl", "dlf", "dK0", "dK1", "dV0", "dV1", "dout",
                                                       "ddbg"]:
            P.new_sem(k)
        ws = WStream(P, wslots)

        bank_rr = [0]

        def bank(lim=8):
            i = bank_rr[0] % lim
            bank_rr[0] += 1
            return psb[i], f"ps{i}"

        def hs(half):
            return slice(half * 512, (half + 1) * 512)

        def mm_group(ps_ap, pairs):
            def fn(e):
                inst = None
                n = len(pairs)
                for i, (l, r) in enumerate(pairs):
                    inst = e.matmul(ps_ap, l, r, start=(i == 0), stop=(i == n - 1))
                return inst
            return fn

        def body():
            bank_rr[0] = 0
            cb = []

            def cload(eng, out, in_, name):
                P.op(eng, lambda e: e.dma_start(out=out, in_=in_), writes=[name], sem="dconst")
                cb.append(name)
            cload("sync", cT, cT_d, "cT")
            cload("sync", bada, bada_d, "bada")
            cload("sync", gains, gains_d, "gains")
            cload("sync", small, small_d, "small")
            cload("sync", wsel, wsel_d, "wsel")
            cload("sync", wown, wown_d, "wown")
            cload("sync", maskc, maskc_d.rearrange("p (r t) -> p r t", r=4), "maskc")
            cload("sync", invc, invc_d.rearrange("p (g t) -> p g t", g=4), "invc")
            cload("sync", ident8, ident_d, "ident8")
            P.join("dconst", cb)
            P.op("gpsimd", lambda e: e.dma_start(out=wf, in_=wf_d), writes=["wf"], sem="dlf")
            for i in range(4):
                P.op("sync", lambda e, i=i: e.dma_start(out=XT[:, 4 * i:4 * i + 4, :], in_=xT_d[:, 4 * i:4 * i + 4, :]),
                     writes=[f"x{c}_{h}" for c in range(4 * i, 4 * i + 4) for h in range(2)], sem=f"dx{i}")
            P.op("vector", lambda e: e.memset(ones_bf, 1.0), writes=["ones_bf"])
            P.op("vector", lambda e: e.memset(ones8, 1.0), writes=["ones8"])
            P.op("vector", lambda e: e.memset(smd[:, 2:3], EPS), writes=["eps"])
            P.op("vector", lambda e: e.tensor_scalar(smd[:, 0:1], small[:, 0:1], float(128 ** -0.5), None, ALU.mult),
                 reads=["small"], writes=["gqs"])
            P.op("vector", lambda e: e.tensor_scalar(smd[:, 1:2], small[:, 2:3], -1.0, None, ALU.mult),
                 reads=["small"], writes=["negb"])
            P.op("scalar", lambda e: e.activation(cact, cT, AF.Silu), reads=["cT"], writes=["cact"])

            ad_next = [0]

            def adaln_step(n=1):
                for _ in range(n):
                    fc = ad_next[0]
                    if fc >= 144:
                        return
                    ad_next[0] += 1
                    w, wn = ws.get(wada_d[fc], 2048)
                    ps, pn = bank()
                    P.op("tensor", mm_group(ps[:, 0:1], [(w[:, kc * 128:(kc + 1) * 128], cact[:, kc:kc + 1])
                                                        for kc in range(16)]),
                         reads=[wn, "cact"], writes=[pn])
                    P.op("vector", lambda e, ps=ps, fc=fc: e.tensor_tensor(modT[:, fc:fc + 1], ps[:, 0:1],
                                                                         bada[:, fc:fc + 1], ALU.add),
                         reads=[pn, "bada"], writes=[f"mod{fc}"])

            def mod_names(k):
                return [f"mod{k * 16 + c}" for c in range(16)]

            def derive_A(slot, sc_piece, gain_idx, name):
                P.op("vector", lambda e: e.scalar_tensor_tensor(
                    der[:, slot * 16:(slot + 1) * 16], modT[:, sc_piece * 16:(sc_piece + 1) * 16], 1.0,
                    gains[:, gain_idx * 16:(gain_idx + 1) * 16], ALU.add, ALU.mult),
                    reads=mod_names(sc_piece) + ["gains"], writes=[name])

            def derive_HG(slot, g_piece, name):
                P.op("vector", lambda e: e.tensor_scalar(
                    der[:, slot * 16:(slot + 1) * 16], modT[:, g_piece * 16:(g_piece + 1) * 16], 0.5, None, ALU.mult),
                    reads=mod_names(g_piece), writes=[name])

            norm_tmp = ["sq0", "sq1", "rstd0", "rstd1", "tt0", "tt1", "lnv"]

            def rms_stats(scale_dim):
                for half in range(2):
                    ps, pn = bank()
                    for c in range(16):
                        i = c % 2
                        P.op("scalar", lambda e, c=c, i=i, half=half: e.activation(sq[i], XT[:, c, hs(half)], AF.Square),
                             reads=[f"x{c}_{half}"], writes=[f"sq{i}"])
                        P.op("tensor", lambda e, c=c, i=i, ps=ps: e.matmul(ps[:], ones_bf, sq[i], start=(c == 0), stop=(c == 15)),
                             reads=[f"sq{i}", "ones_bf"], acc=[pn])
                    P.op("scalar", lambda e, ps=ps: e.activation(lnv, ps[:], AF.Ln, bias=smd[:, 2:3], scale=1.0 / scale_dim),
                         reads=[pn, "eps"], writes=["lnv"])
                    P.op("scalar", lambda e, half=half: e.activation(rstd[:, hs(half)], lnv, AF.Exp, scale=-0.5),
                         reads=["lnv"], writes=[f"rstd{half}"])

            def norm_mod(Acols, Anames, Bcols, Bnames):
                rms_stats(D)
                for half in range(2):
                    for c in range(16):
                        i = c % 2
                        P.op("vector", lambda e, c=c, i=i, half=half: e.tensor_tensor(tt[i], XT[:, c, hs(half)], rstd[:, hs(half)], ALU.mult),
                             reads=[f"x{c}_{half}", f"rstd{half}"], writes=[f"tt{i}"])
                        P.op("scalar", lambda e, c=c, i=i, half=half: e.activation(
                            hT[:, c, hs(half)], tt[i], AF.Identity, bias=Bcols[:, c:c + 1], scale=Acols[:, c:c + 1]),
                            reads=[f"tt{i}"] + Anames + Bnames, writes=[f"h{c}_{half}"])

            def ffn(win_d, wout_d, HGcols, HGname, hook):
                for G, (f0, nf) in enumerate(GROUPS):
                    for fl in range(nf):
                        f = f0 + fl
                        pss = {}
                        for ab in range(2):
                            w, wn = ws.get(win_d[2 * f + ab], 2048)
                            for half in range(2):
                                ps, pn = bank()
                                pss[(ab, half)] = (ps, pn)
                                P.op("tensor", mm_group(ps[:], [(w[:, kc * 128:(kc + 1) * 128], hT[:, kc, hs(half)])
                                                                for kc in range(16)]),
                                     reads=[wn] + [f"h{kc}_{half}" for kc in range(16)], writes=[pn])
                        for half in range(2):
                            pa, pan = pss[(0, half)]
                            pb, pbn = pss[(1, half)]
                            i = half
                            P.op("scalar", lambda e, pa=pa, i=i: e.activation(sa[i], pa[:], AF.Silu),
                                 reads=[pan], writes=[f"sa{i}"])
                            P.op("vector", lambda e, pb=pb, i=i, fl=fl, half=half: e.tensor_tensor(
                                gT[:, fl, hs(half)], sa[i], pb[:], ALU.mult),
                                reads=[f"sa{i}", pbn], writes=[f"g{fl}_{half}"])
                        hook()
                    if G == 0:
                        hook(final=True)
                    for dc in range(16):
                        w, wn = ws.get(wout_d[G * 16 + dc, :, 0:nf * 128], nf * 128)
                        for half in range(2):
                            ps, pn = bank()
                            P.op("tensor", mm_group(ps[:], [(w[:, fl * 128:(fl + 1) * 128], gT[:, fl, hs(half)])
                                                            for fl in range(nf)]),
                                 reads=[wn] + [f"g{fl}_{half}" for fl in range(nf)], writes=[pn])
                            P.op("vector", lambda e, ps=ps, dc=dc, half=half: e.scalar_tensor_tensor(
                                XT[:, dc, hs(half)], ps[:], HGcols[:, dc:dc + 1], XT[:, dc, hs(half)], ALU.mult, ALU.add),
                                reads=[pn, HGname, f"x{dc}_{half}"], writes=[f"x{dc}_{half}"])
                        hook()

            def dbg_dump(k):
                if DEBUG:
                    P.op("sync", lambda e: e.dma_start(out=dbg_d[k], in_=XT),
                         reads=[f"x{c}_{h}" for c in range(16) for h in range(2)], sem="ddbg")

            def store_out():
                for i in range(4):
                    P.op("sync", lambda e, i=i: e.dma_start(out=outT_d[:, 4 * i:4 * i + 4, :], in_=XT[:, 4 * i:4 * i + 4, :]),
                         reads=[f"x{cc}_{h}" for cc in range(4 * i, 4 * i + 4) for h in range(2)], sem="dout")

            def checkpoint(k):
                if STOP == k:
                    store_out()
                    raise _StopBody()

            checkpoint(0)
            adaln_step(32)
            derive_A(0, 1, 0, "A1")
            norm_mod(der[:, 0:16], ["A1"], modT[:, 0:16], mod_names(0))

            def hook1(final=False):
                if final:
                    while ad_next[0] < 48:
                        adaln_step(1)
                    derive_HG(1, 2, "HG1")
                else:
                    adaln_step(1)
            ffn(w1in_d, w1out_d, der[:, 16:32], "HG1", hook1)
            adaln_step(144)
            dbg_dump(0)
            checkpoint(1)

            derive_A(2, 4, 1, "A2")
            toks = P.retire([f"g{fl}_{h}" for fl in range(11) for h in range(2)] + ["sa0", "sa1"])
            norm_mod(der[:, 32:48], ["A2"], modT[:, 48:64], mod_names(3))
            t4 = P.retire(norm_tmp)
            P.seed(["st0", "st1", "LFown"], t4)
            P.seed(["sqk0", "sqk1", "rk0", "rk1"], toks)
            for half in range(2):
                ps, pn = bank()
                P.op("tensor", mm_group(ps[0:8, :], [(wf[:, kc * 8:(kc + 1) * 8], hT[:, kc, hs(half)]) for kc in range(16)]),
                     reads=["wf"] + [f"h{kc}_{half}" for kc in range(16)], writes=[pn])
                lf = LFown[0:8, hs(half)]
                P.op("scalar", lambda e, ps=ps, lf=lf: e.activation(lf, ps[0:8, :], AF.Exp, bias=smd[0:8, 1:2], scale=-1.0),
                     reads=[pn, "negb"], writes=["LFown"])
                P.op("vector", lambda e, lf=lf: e.tensor_scalar(lf, lf, 1.0, None, ALU.add), reads=["LFown"], writes=["LFown"])
                P.op("scalar", lambda e, lf=lf: e.activation(lf, lf, AF.Ln), reads=["LFown"], writes=["LFown"])
                P.op("vector", lambda e, lf=lf: e.tensor_scalar(lf, lf, -1.0, None, ALU.mult), reads=["LFown"], writes=["LFown"])
            P.op("sync", lambda e: e.dma_start(out=agf_in.ap()[128:136, :], in_=LFown[0:8, :]),
                 reads=["LFown"], writes=["agf_lf"], sem="dagf")
            for c in range(8):
                w, wn = ws.get(wqkvu_d[24 + c], 2048)
                for half in range(2):
                    ps, pn = bank()
                    P.op("tensor", mm_group(ps[:], [(w[:, kc * 128:(kc + 1) * 128], hT[:, kc, hs(half)]) for kc in range(16)]),
                         reads=[wn] + [f"h{kc}_{half}" for kc in range(16)], writes=[pn])
                    P.op("scalar", lambda e, ps=ps, c=c, half=half: e.activation(
                        uext[:, c * 8 + 4 * half:c * 8 + 4 * half + 4, 16:144],
                        ps[:].rearrange("p (a b) -> p a b", b=128), AF.Copy),
                        reads=[pn], writes=[f"u{c}_{half}"])
            unames = [f"u{c}_{h}" for c in range(8) for h in range(2)]
            P.op("sync", lambda e: e.dma_start(out=agf_in.ap()[0:128, :].rearrange("p (x s) -> p x s", s=16),
                                               in_=uext[:, :, 128:144]),
                 reads=unames, writes=["agf_u"], sem="dagf")
            P.op("gpsimd", lambda e: e.collective_compute(
                "AllGather", ALU.bypass, replica_groups=[[0, 1, 2, 3], [4, 5, 6, 7]],
                ins=[agf_in.ap().opt()], outs=[agf_out.ap().opt()]),
                reads=["agf_lf", "agf_u"], writes=["agf_out"])

            checkpoint(2)
            def qk_proj(tile0, gcol, gname, dst_fn, dst_name_fn, after_head):
                pending = []

                def flush():
                    while pending:
                        pending.pop(0)()

                for h in range(8):
                    w, wn = ws.get(wqkvu_d[tile0 + h], 2048)
                    for half in range(2):
                        ps, pn = bank()
                        P.op("tensor", mm_group(ps[:], [(w[:, kc * 128:(kc + 1) * 128], hT[:, kc, hs(half)]) for kc in range(16)]),
                             reads=[wn] + [f"h{kc}_{half}" for kc in range(16)], writes=[pn])
                        flush()

                        def tail(ps=ps, pn=pn, h=h, half=half):
                            i = half
                            P.op("scalar", lambda e: e.activation(sqk[i], ps[:], AF.Square),
                                 reads=[pn], writes=[f"sqk{i}"])
                            ps2, pn2 = bank()
                            P.op("tensor", lambda e: e.matmul(ps2[:], ones_bf, sqk[i], start=True, stop=True),
                                 reads=[f"sqk{i}", "ones_bf"], writes=[pn2])
                            P.op("scalar", lambda e: e.activation(rk[i], ps2[:], AF.Ln, bias=smd[:, 2:3], scale=1.0 / 128),
                                 reads=[pn2, "eps"], writes=[f"rk{i}"])
                            P.op("scalar", lambda e: e.activation(rk[i], rk[i], AF.Exp, scale=-0.5),
                                 reads=[f"rk{i}"], writes=[f"rk{i}"])
                            P.op("vector", lambda e: e.scalar_tensor_tensor(
                                dst_fn(h, half), ps[:], gcol, rk[i], ALU.mult, ALU.mult),
                                reads=[pn, f"rk{i}", gname], writes=[dst_name_fn(h, half)])
                            if half == 1:
                                after_head(h)
                        pending.append(tail)
                flush()

            def k_after(h):
                if h % 2 == 1:
                    s = (h // 2) % 2
                    h0 = h - 1
                    dst = agkv_in[h0 // 4].ap().rearrange("(h a) (b t) -> (a b) h t", a=8, b=16)[:, h0 % 4:h0 % 4 + 2, :]
                    P.op("sync", lambda e, s=s, dst=dst: e.dma_start(out=dst, in_=kst[s]),
                         reads=[f"st{s}"], writes=[f"agk{h0}"], sem=f"dst{s}")
                    if h % 4 == 3:
                        q = h // 4
                        P.op("gpsimd", lambda e, q=q: e.collective_compute(
                            "AllGather", ALU.bypass, replica_groups=[[0, 1, 2, 3], [4, 5, 6, 7]],
                            ins=[agkv_in[q].ap().opt()], outs=[agkv_out[q].ap().opt()]),
                            reads=[f"agk{h - 3}", f"agk{h - 1}"], writes=[f"agkv_out{q}"])
            qk_proj(8, small[:, 1:2], "small", lambda h, half: kst[(h // 2) % 2][:, h % 2, hs(half)],
                    lambda h, half: f"st{(h // 2) % 2}", k_after)
            vi = [0]
            for hv in range(8):
                w, wn = ws.get(wqkvu_d[16 + hv], 2048)
                for mq in range(2):
                    ps, pn = bank()

                    def vfn(e, ps=ps, w=w, mq=mq):
                        inst = None
                        for mm in range(4):
                            m = mq * 4 + mm
                            for kc in range(16):
                                inst = e.matmul(ps[:, mm * 128:(mm + 1) * 128], hT[:, kc, m * 128:(m + 1) * 128],
                                                w[:, kc * 128:(kc + 1) * 128], start=(kc == 0), stop=(kc == 15))
                        return inst
                    P.op("tensor", vfn, reads=[wn] + [f"h{kc}_{mq}" for kc in range(16)], writes=[pn])
                    s = vi[0] % 2
                    vi[0] += 1
                    eng = "scalar" if s == 0 else "vector"
                    if eng == "scalar":
                        P.op("scalar", lambda e, ps=ps, s=s: e.activation(vst[s], ps[:].rearrange("p (m d) -> p m d", d=128), AF.Copy),
                             reads=[pn], writes=[f"st{s}"])
                    else:
                        P.op("vector", lambda e, ps=ps, s=s: e.tensor_copy(vst[s], ps[:].rearrange("p (m d) -> p m d", d=128)),
                             reads=[pn], writes=[f"st{s}"])
                    dst = agkv_in[2 + hv // 4].ap().rearrange("(h m) (i d) -> i h m d", m=8, d=128)[:, hv % 4, mq * 4:(mq + 1) * 4, :]
                    P.op("sync", lambda e, s=s, dst=dst: e.dma_start(out=dst, in_=vst[s]),
                         reads=[f"st{s}"], writes=[f"agv{hv}_{mq}"], sem=f"dst{s}")
                if hv % 4 == 3:
                    q = 2 + hv // 4
                    P.op("gpsimd", lambda e, q=q: e.collective_compute(
                        "AllGather", ALU.bypass, replica_groups=[[0, 1, 2, 3], [4, 5, 6, 7]],
                        ins=[agkv_in[q].ap().opt()], outs=[agkv_out[q].ap().opt()]),
                        reads=[f"agv{hh}_{mq}" for hh in range(hv - 3, hv + 1) for mq in range(2)],
                        writes=[f"agkv_out{q}"])
            t4 = P.retire(["st0", "st1", "LFown"])
            P.seed([f"q{h}_{half}" for h in range(8) for half in range(2)], t4)
            qk_proj(0, smd[:, 0:1], "gqs", lambda h, half: qnT[:, h, hs(half)], lambda h, half: f"q{h}_{half}",
                    lambda h: None)

            checkpoint(3)
            t2 = P.retire([f"h{c}_{h}" for c in range(16) for h in range(2)])
            P.seed(["TL0", "TL1", "TL2", "TL3", "TL3z", "T1", "T2", "pl0", "pl1", "ptmp"], t2)
            t3 = P.retire(["sqk0", "sqk1", "rk0", "rk1"])
            P.seed([f"mx{c}_{h}" for c in range(16) for h in range(2)], t3)
            agf3 = agf_out.ap().rearrange("(r x) c -> r x c", r=4)
            for r in range(3):
                P.op("sync", lambda e, r=r: e.dma_start(out=TL[:, r], in_=agf3[r, 0:128, :].rearrange("p (x s) -> p x s", s=16)),
                     reads=["agf_out"], writes=[f"TL{r}"], sem="dtl")
            TL3v = TL[:, 3].rearrange("p (c m) s -> p c m s", m=8)
            P.op("vector", lambda e: e.memset(TL3v[:, :, 0, :], 0.0), writes=["TL3z"])
            for c in range(8):
                P.op("sync", lambda e, c=c: e.dma_start(
                    out=TL3v[:, c, 1:8, :],
                    in_=agf3[3, 0:128, :].rearrange("p (c m s) -> p c m s", m=8, s=16)[:, c, 0:7, :]),
                    reads=["agf_out"], writes=["TL3"], sem="dtl")
            P.join("dtl", ["TL0", "TL1", "TL2", "TL3"])
            halo = uext[:, :, 0:16]
            P.op("vector", lambda e: e.tensor_scalar(halo, TL[:, 0], wsel[:, 0:1], None, ALU.mult),
                 reads=["TL0", "wsel"], writes=["halo"])
            for r in range(1, 4):
                P.op("vector", lambda e, r=r: e.scalar_tensor_tensor(halo, TL[:, r], wsel[:, r:r + 1], halo, ALU.mult, ALU.add),
                     reads=[f"TL{r}", "TL3z", "wsel", "halo"], writes=["halo"])
            for c in range(8):
                g = c // 2
                wwin = 2 ** (g + 1)
                e_ = uext[:, c * 8:(c + 1) * 8, :]
                cur, curn = e_, None
                bufs = [(T1, "T1"), (T2, "T2")]
                sh = 1
                for lvl in range(g + 1):
                    dst, dn = bufs[lvl % 2]
                    lo = 2 * sh - 1
                    rd = [f"u{c}_0", f"u{c}_1", "halo"] if curn is None else [curn]
                    P.op("vector", lambda e, dst=dst, cur=cur, lo=lo, sh=sh: e.tensor_tensor(
                        dst[:, :, lo:144], cur[:, :, lo:144], cur[:, :, lo - sh:144 - sh], ALU.add),
                        reads=rd, writes=[dn])
                    cur, curn = dst, dn
                    sh *= 2
                cc = c % 2
                P.op("vector", lambda e, cur=cur, e_=e_, cc=cc, wwin=wwin: e.scalar_tensor_tensor(
                    pooled[:, cc, :].rearrange("p (m i) -> p m i", i=128), cur[:, :, 16:144], 1.0 / wwin,
                    e_[:, :, 16:144], ALU.mult, ALU.subtract),
                    reads=[curn, f"u{c}_0", f"u{c}_1"], writes=[f"pl{cc}"])
                P.op("vector", lambda e, cur=cur, g=g: e.tensor_tensor(ptmp, cur[:, 0, 16:144], invc[:, g, :], ALU.mult),
                     reads=[curn, "invc"], writes=["ptmp"])
                P.op("vector", lambda e, e_=e_, cc=cc: e.tensor_tensor(pooled[:, cc, 0:128], ptmp, e_[:, 0, 16:144], ALU.subtract),
                     reads=["ptmp", f"u{c}_0", f"pl{cc}"], writes=[f"pl{cc}"])
                if cc == 1:
                    w, wn = ws.get(poolw_d[g], 512)
                    for dh in range(2):
                        for half in range(2):
                            ps, pn = bank()
                            P.op("tensor", mm_group(ps[:], [(w[:, k2 * 256 + dh * 128:k2 * 256 + dh * 128 + 128],
                                                             pooled[:, k2, hs(half)]) for k2 in range(2)]),
                                 reads=[wn, "pl0", "pl1"], writes=[pn])
                            col = 8 + 2 * g + dh
                            P.op("scalar", lambda e, ps=ps, col=col, half=half: e.activation(
                                mixT[:, col, hs(half)], ps[:], AF.Identity, scale=small[:, col:col + 1]),
                                reads=[pn, "small"], writes=[f"mx{col}_{half}"])

            checkpoint(4)
            t5 = P.retire(unames + ["halo"])
            P.seed(["LF", "Fg", "ones1k"], t5)
            t2 = P.retire(["TL0", "TL1", "TL2", "TL3", "TL3z", "T1", "T2", "pl0", "pl1", "ptmp"])
            P.seed(["K0", "K1", "V0", "V1", "Fown", "negFT", "Fbc0", "Fbc1"], t2)
            LFv = LF[0:8, :].rearrange("p (m r i) -> p m r i", r=4, i=128)
            for r in range(4):
                P.op("sync", lambda e, r=r: e.dma_start(out=LFv[:, :, r, :],
                                                        in_=agf3[r, 128:136, :].rearrange("p (m i) -> p m i", i=128)),
                     reads=["agf_out"], writes=["LF"], sem="dlf")
            P.join("dlf", ["LF", "wf"])
            P.op("vector", lambda e: e.memset(ones1k[0:8, :], 1.0), writes=["ones1k"])
            for qd in range(4):
                init = 0.0 if qd == 0 else Fg[0:8, qd * 1024 - 1:qd * 1024]
                P.op("vector", lambda e, qd=qd, init=init: e.tensor_tensor_scan(
                    Fg[0:8, qd * 1024:(qd + 1) * 1024], ones1k[0:8, :], LF[0:8, qd * 1024:(qd + 1) * 1024],
                    init, ALU.mult, ALU.add),
                    reads=["LF", "ones1k", "Fg"], writes=["Fg"])
            Fgv = Fg[0:8, :].rearrange("p (m r i) -> p m r i", r=4, i=128)
            Fownv = Fown[0:8, :].rearrange("p (m i) -> p m i", i=128)
            P.op("vector", lambda e: e.tensor_scalar(Fownv, Fgv[:, :, 0, :], wown[0:8, 0:1], None, ALU.mult),
                 reads=["Fg", "wown"], writes=["Fown"])
            for r in range(1, 4):
                P.op("vector", lambda e, r=r: e.scalar_tensor_tensor(Fownv, Fgv[:, :, r, :], wown[0:8, r:r + 1], Fownv,
                                                                    ALU.mult, ALU.add),
                     reads=["Fg", "wown", "Fown"], writes=["Fown"])
            ps, pn = bank()

            def trf(e, ps=ps):
                inst = None
                for jk in range(32):
                    inst = e.transpose(ps[:, jk * 8:(jk + 1) * 8], Fg[0:8, jk * 128:(jk + 1) * 128], ident8[0:8, 0:8])
                return inst
            P.op("tensor", trf, reads=["Fg", "ident8"], writes=[pn])
            P.op("vector", lambda e, ps=ps: e.tensor_scalar(negFT, ps[:, 0:256], -1.0, None, ALU.mult),
                 reads=[pn], writes=["negFT"])

            checkpoint(5)
            t5 = P.retire(["LF", "Fg", "ones1k"])
            P.seed(["Sb0", "Sb1", "Sb2", "PT0", "PT1", "PT2", "PT3", "OL0", "OL1", "rl", "Fm"], t5)
            Ksrcs = [agkv_out[q].ap().rearrange("(r x) c -> r x c", r=4).rearrange("r (h a) (b t) -> h (a b) r t", a=8, b=16)
                     for q in range(2)]
            Vsrcs = [agkv_out[2 + q].ap().rearrange("(r x) c -> r x c", r=4).rearrange("r (h m) (i d) -> h i r m d", m=8, d=128)
                     for q in range(2)]
            psO = [(psb[4], "ps4"), (psb[5], "ps5")]
            psL = [(psb[6], "ps6"), (psb[7], "ps7")]
            LA = 2
            NSB, NPT = 3, 4
            chunks = []
            for h in range(8):
                for mh in range(2):
                    for mm in range(4):
                        m = mh * 4 + mm
                        for r in range(4):
                            jk = 4 * m + r
                            c0 = m * 128
                            cl = [(c0, 512, 0), (512, 1024, 1)] if c0 < 512 else [(c0, 1024, 1)]
                            for ci, (a, b, bk) in enumerate(cl):
                                chunks.append(dict(h=h, mh=mh, mm=mm, r=r, jk=jk, a=a, b=b, bk=bk, c0=c0,
                                                   lastc=(ci == len(cl) - 1), idx=len(chunks)))
            seen_h = set()
            seen_seg = set()

            def stage_a(ch):
                h, mh, mm, r, jk, a, b, c0 = ch["h"], ch["mh"], ch["mm"], ch["r"], ch["jk"], ch["a"], ch["b"], ch["c0"]
                fb = h % 2
                s = (2 * h + mh) % 2
                if h not in seen_h:
                    seen_h.add(h)
                    P.op("vector", lambda e, h=h: e.tensor_scalar(Fm[0:8, :], Fown[0:8, :], ident8[0:8, h:h + 1], None, ALU.mult),
                         reads=["Fown", "ident8"], writes=["Fm"])
                    for half in range(2):
                        ps, pn = bank(4)
                        P.op("tensor", lambda e, ps=ps, half=half: e.matmul(ps[:], ones8[0:8, :], Fm[0:8, hs(half)], start=True, stop=True),
                             reads=["Fm", "ones8"], writes=[pn])
                        P.op("scalar", lambda e, ps=ps, half=half, fb=fb: e.activation(Fbc[fb][:, hs(half)], ps[:], AF.Copy),
                             reads=[pn], writes=[f"Fbc{fb}"])
                if (h, mh) not in seen_seg:
                    seen_seg.add((h, mh))
                    P.op("sync", lambda e, s=s, h=h, mh=mh: e.dma_start(out=Ksl[s], in_=Ksrcs[h // 4][h % 4][:, :, mh * 512:(mh + 1) * 512]),
                         reads=[f"agkv_out{h // 4}"], writes=[f"K{s}"], sem=f"dK{s}")
                    for rr in range(4):
                        P.op("sync", lambda e, s=s, h=h, mh=mh, rr=rr: e.dma_start(
                            out=Vsl[s][:, rr], in_=Vsrcs[h // 4][h % 4][:, rr, mh * 4:(mh + 1) * 4, :]),
                            reads=[f"agkv_out{2 + h // 4}"], writes=[f"V{s}"], sem=f"dV{s}")
                    P.join(f"dV{s}", [f"V{s}"])
                n = b - a
                ps, pn = bank(4)
                P.op("tensor", lambda e, ps=ps, s=s, r=r, mm=mm, h=h, a=a, b=b, n=n: e.matmul(
                    ps[:, 0:n], Ksl[s][:, r, mm * 128:(mm + 1) * 128], qnT[:, h, a:b], start=True, stop=True),
                    reads=[f"K{s}", f"q{h}_0", f"q{h}_1"], writes=[pn])
                si = ch["idx"] % NSB
                P.op("vector", lambda e, ps=ps, si=si, n=n, jk=jk, h=h, fb=fb, a=a, b=b: e.scalar_tensor_tensor(
                    Sb[si][:, 0:n], ps[:, 0:n], negFT[:, jk * 8 + h:jk * 8 + h + 1], Fbc[fb][:, a:b],
                    ALU.add, ALU.add),
                    reads=[pn, "negFT", f"Fbc{fb}"], writes=[f"Sb{si}"])
                if a == c0:
                    P.op("vector", lambda e, si=si, r=r: e.tensor_tensor(
                        Sb[si][:, 0:128], Sb[si][:, 0:128], maskc[:, r, :], ALU.min),
                        reads=[f"Sb{si}", "maskc"], writes=[f"Sb{si}"])
                pi = ch["idx"] % NPT
                P.op("scalar", lambda e, si=si, pi=pi, n=n: e.activation(PT[pi][:, 0:n], Sb[si][:, 0:n], AF.Exp),
                     reads=[f"Sb{si}"], writes=[f"PT{pi}"])

            def stage_b(ch):
                h, mh, mm, r, jk, a, b, bk = ch["h"], ch["mh"], ch["mm"], ch["r"], ch["jk"], ch["a"], ch["b"], ch["bk"]
                s = (2 * h + mh) % 2
                n = b - a
                pi = ch["idx"] % NPT
                lastjk = 15 if bk == 0 else 31
                oa, ob = a - 512 * bk, b - 512 * bk

                def pvf(e, pi=pi, n=n, s=s, r=r, mm=mm, bk=bk, oa=oa, ob=ob, jk=jk, lastjk=lastjk):
                    e.matmul(psO[bk][0][:, oa:ob], Vsl[s][:, r, mm, :], PT[pi][:, 0:n],
                             start=(jk == 0), stop=(jk == lastjk))
                    return e.matmul(psL[bk][0][:, oa:ob], ones_bf, PT[pi][:, 0:n],
                                    start=(jk == 0), stop=(jk == lastjk))
                P.op("tensor", pvf, reads=[f"V{s}", f"PT{pi}", "ones_bf"], acc=[psO[bk][1], psL[bk][1]])
                if ch["lastc"]:
                    for bk2 in range(2):
                        if jk == (15 if bk2 == 0 else 31):
                            P.op("scalar", lambda e, bk2=bk2: e.activation(OLs[0], psO[bk2][0][:], AF.Copy),
                                 reads=[psO[bk2][1]], writes=["OL0"])
                            P.op("scalar", lambda e, bk2=bk2: e.activation(OLs[1], psL[bk2][0][:], AF.Copy),
                                 reads=[psL[bk2][1]], writes=["OL1"])
                            P.op("vector", lambda e: e.reciprocal(rl, OLs[1]), reads=["OL1"], writes=["rl"])
                            P.op("vector", lambda e, h=h, bk2=bk2: e.tensor_tensor(mixT[:, h, hs(bk2)], OLs[0], rl, ALU.mult),
                                 reads=["OL0", "rl"], writes=[f"mx{h}_{bk2}"])

            nch = len(chunks)
            for i in range(nch + LA):
                if i < nch:
                    stage_a(chunks[i])
                if i - LA >= 0:
                    stage_b(chunks[i - LA])

            checkpoint(6)
            for dc in range(16):
                w, wn = ws.get(womix_d[dc], 2048)
                for half in range(2):
                    ps, pn = bank()
                    P.op("tensor", mm_group(ps[:], [(w[:, kc * 128:(kc + 1) * 128], mixT[:, kc, hs(half)]) for kc in range(16)]),
                         reads=[wn] + [f"mx{kc}_{half}" for kc in range(16)], writes=[pn])
                    P.op("vector", lambda e, ps=ps, dc=dc, half=half: e.scalar_tensor_tensor(
                        XT[:, dc, hs(half)], ps[:], modT[:, 80 + dc:81 + dc], XT[:, dc, hs(half)], ALU.mult, ALU.add),
                        reads=[pn, f"mod{80 + dc}", f"x{dc}_{half}"], writes=[f"x{dc}_{half}"])
            dbg_dump(1)
            checkpoint(7)

            derive_A(3, 7, 2, "A3")
            derive_HG(4, 8, "HG3")
            t4 = P.retire([f"q{h}_{half}" for h in range(8) for half in range(2)])
            P.seed(norm_tmp, t4)
            t2 = P.retire(["K0", "K1", "V0", "V1", "Fown", "negFT", "Fbc0", "Fbc1"])
            P.seed([f"h{c}_{h}" for c in range(16) for h in range(2)], t2)
            t3 = P.retire([f"mx{c}_{h}" for c in range(16) for h in range(2)])
            P.seed([f"g{fl}_{h}" for fl in range(11) for h in range(2)] + ["sa0", "sa1"], t3)
            norm_mod(der[:, 48:64], ["A3"], modT[:, 96:112], mod_names(6))
            ffn(w2in_d, w2out_d, der[:, 64:80], "HG3", lambda final=False: None)
            dbg_dump(2)

            rms_stats(D)
            for c in range(16):
                for half in range(2):
                    P.op("vector", lambda e, c=c, half=half: e.scalar_tensor_tensor(
                        XT[:, c, hs(half)], XT[:, c, hs(half)], gains[:, 48 + c:49 + c], rstd[:, hs(half)], ALU.mult, ALU.mult),
                        reads=[f"x{c}_{half}", f"rstd{half}", "gains"], writes=[f"x{c}_{half}"])
                if c % 4 == 3:
                    i = c // 4
                    P.op("sync", lambda e, i=i: e.dma_start(out=outT_d[:, 4 * i:4 * i + 4, :], in_=XT[:, 4 * i:4 * i + 4, :]),
                         reads=[f"x{cc}_{h}" for cc in range(4 * i, 4 * i + 4) for h in range(2)], sem="dout")

        P.dry = True
        try:
            body()
        except _StopBody:
            pass
        P.dry = False
        P.reset()
        ws.start_real()
        try:
            body()
        except _StopBody:
            pass
        fw = [("dout", P.cnt["dout"])]
        if DEBUG:
            fw.append(("ddbg", P.cnt["ddbg"]))
        P.emit(final_waits=fw)
    return nc


_NC_CACHE = {}


def _tile_kn(w, col0, ncols):
    K = w.shape[0]
    sub = w[:, col0:col0 + ncols]
    nt = ncols // 128
    a = sub.reshape(K // 128, 128, nt, 128)
    return np.ascontiguousarray(a.transpose(2, 1, 0, 3)).reshape(nt, 128, (K // 128) * 128)


def _prep_shared(inp):
    f32 = np.float32
    sh = {}
    w_ada = np.asarray(inp["w_ada"], f32)[0]
    sh["wada"] = _tile_kn(w_ada, 0, 9 * D)
    sh["badaT"] = np.ascontiguousarray(np.asarray(inp["b_ada"], f32)[0].reshape(144, 128).T)
    gains = np.stack([np.asarray(inp["ffn1_norm_g"], f32)[0], np.asarray(inp["mix_norm_g"], f32)[0],
                      np.asarray(inp["ffn2_norm_g"], f32)[0], np.asarray(inp["final_norm_g"], f32)], 0)
    sh["gainsT"] = np.ascontiguousarray(gains.reshape(4, 16, 128).transpose(2, 0, 1).reshape(128, 64))
    for nm, key_in, key_out in (("1", "ffn1_w_in", "ffn1_w_out"), ("2", "ffn2_w_in", "ffn2_w_out")):
        win = np.asarray(inp[key_in], f32)[0]
        ta = _tile_kn(win, 0, DFF)
        tb = _tile_kn(win, DFF, DFF)
        sh[f"w{nm}in"] = np.ascontiguousarray(np.stack([ta, tb], 1).reshape(86, 128, 2048))
        wout = np.asarray(inp[key_out], f32)[0]
        tiles = np.zeros((64, 128, 1408), f32)
        for G, (f0, nf) in enumerate(GROUPS):
            blk = wout[f0 * 128:(f0 + nf) * 128].reshape(nf, 128, 16, 128)
            tiles[G * 16:(G + 1) * 16, :, 0:nf * 128] = blk.transpose(2, 1, 0, 3).reshape(16, 128, nf * 128)
        sh[f"w{nm}out"] = tiles
    w_in = np.asarray(inp["w_in"], f32)[0]
    sh["wqkvu"] = np.ascontiguousarray(np.concatenate(
        [_tile_kn(w_in, 0, 1024), _tile_kn(w_in, 1024, 1024), _tile_kn(w_in, 2048, 1024), _tile_kn(w_in, 3080, 1024)], 0))
    wfc = w_in[:, 3072:3080].reshape(16, 128, 8)
    sh["wf"] = np.ascontiguousarray(wfc.transpose(1, 0, 2).reshape(128, 128))
    pw = np.asarray(inp["pool_w"], f32)[0]
    sh["poolw"] = np.ascontiguousarray(pw.reshape(4, 2, 128, 256).transpose(0, 2, 1, 3).reshape(4, 128, 512))
    sh["womix"] = _tile_kn(np.asarray(inp["w_out"], f32)[0], 0, D)
    small = np.zeros((128, 16), f32)
    small[:, 0] = np.asarray(inp["q_norm_g"], f32)[0]
    small[:, 1] = np.asarray(inp["k_norm_g"], f32)[0]
    small[0:8, 2] = np.asarray(inp["b_forget"], f32)[0]
    small[:, 8:16] = np.asarray(inp["pool_scale"], f32)[0].reshape(8, 128).T
    sh["small"] = small
    sh["ident8"] = np.ascontiguousarray(np.eye(128, 8, dtype=f32))
    return sh


def _prep_core(inp, core):
    f32 = np.float32
    b, j = core // 4, core % 4
    x = np.asarray(inp["x"], f32)[b]
    xs = x.reshape(8, 4, 128, D)[:, j].reshape(T, D)
    d = {}
    d["xT"] = np.ascontiguousarray(xs.T.reshape(16, 128, T).transpose(1, 0, 2))
    d["cT"] = np.ascontiguousarray(np.asarray(inp["c"], f32)[b].reshape(16, 128).T)
    wsel = np.zeros((128, 4), f32)
    wsel[:, (j - 1) % 4] = 1.0
    d["wsel_halo"] = wsel
    mask = np.zeros((128, 4, 128), f32)
    BIG = 3.0e38
    for r in range(4):
        if r < j:
            mask[:, r, :] = BIG
        elif r == j:
            s_ = np.arange(128)[:, None]
            t_ = np.arange(128)[None, :]
            mask[:, r, :] = np.where(s_ <= t_, BIG, -30000.0)
        else:
            mask[:, r, :] = -30000.0
    d["maskc"] = mask.reshape(128, 512)
    invc = np.zeros((128, 4, 128), f32)
    for g in range(4):
        w = 2 ** (g + 1)
        if j == 0:
            invc[:, g, :] = 1.0 / np.minimum(np.arange(128) + 1, w)
        else:
            invc[:, g, :] = 1.0 / w
    d["invc"] = invc.reshape(128, 512)
    return d


def kernel(**inp):
    if "nc" not in _NC_CACHE:
        _NC_CACHE["nc"] = build_nc()
    nc = _NC_CACHE["nc"]
    sh = _prep_shared(inp)
    in_maps = []
    for core in range(8):
        d = _prep_core(inp, core)
        j = core % 4
        m = dict(sh)
        m["xT"] = d["xT"]
        m["cT"] = d["cT"]
        m["maskc"] = d["maskc"]
        m["invc"] = d["invc"]
        m["wsel"] = d["wsel_halo"]
        ws2 = np.zeros((128, 4), np.float32)
        ws2[:, j] = 1.0
        m["wsel_own"] = ws2
        in_maps.append(m)
    res = run_bass_kernel_spmd(nc, in_maps, core_ids=list(range(8)))
    out = np.empty((2, 4096, D), np.float32)
    for core in range(8):
        b, j = core // 4, core % 4
        oT = res.results[core]["outT"]
        loc = oT.transpose(2, 1, 0).reshape(T, D)
        out[b].reshape(8, 4, 128, D)[:, j] = loc.reshape(8, 128, D)
    if DEBUG:
        kernel.last_dbg = [res.results[c]["dbg"] for c in range(8)]
    return out
```

```python
import numpy as np
import concourse.bass as bass
import concourse.mybir as mybir
from concourse.bass_utils import run_bass_kernel_spmd
from contextlib import ExitStack

F32 = mybir.dt.float32
BF16 = mybir.dt.bfloat16
AF = mybir.ActivationFunctionType
ALU = mybir.AluOpType

ENGS = ["tensor", "vector", "scalar", "gpsimd", "sync"]
D = 2048
T = 1024
NCH = 16
DFF = 5504
NF = 43
GROUPS = [(0, 11), (11, 11), (22, 11), (33, 10)]
NSLOT = 4
EPS = 1e-6
DEBUG = False
STOP = 99


class _StopBody(Exception):
    pass


class Prog:
    def __init__(self, nc, es):
        self.nc = nc
        self.es = es
        self.dry = False
        self.sems = {}
        self.reset()

    def reset(self):
        self.q = {e: [] for e in ENGS}
        self.cnt = {k: 0 for k in self.sems}
        self.waited = {}
        self.last_w = {}
        self.readers = {}

    def new_sem(self, key):
        if key not in self.sems:
            self.sems[key] = self.es.enter_context(self.nc.semaphore(key))
        self.cnt[key] = 0
        return key

    def op(self, eng, fn, reads=(), writes=(), acc=(), deps=(), sem=None):
        if self.dry:
            return None
        mykey = "E_" + eng
        alld = list(deps)
        for r in reads:
            if r in self.last_w:
                alld.append(self.last_w[r])
        for w in writes:
            if w in self.last_w:
                alld.append(self.last_w[w])
            alld.extend(self.readers.get(w, ()))
        for w in acc:
            if w in self.last_w and self.last_w[w][0] != mykey:
                alld.append(self.last_w[w])
            alld.extend(self.readers.get(w, ()))
        need = {}
        for (k, v) in alld:
            need[k] = max(need.get(k, 0), v)
        waits = []
        for k, v in need.items():
            if v > self.waited.get((eng, k), 0):
                self.waited[(eng, k)] = v
                waits.append((k, v))
        if sem is None:
            key, inc = mykey, 1
        else:
            key, inc = sem, 16
        self.cnt[key] += inc
        tok = (key, self.cnt[key])
        self.q[eng].append((waits, fn, key, inc))
        for r in reads:
            self.readers.setdefault(r, []).append(tok)
        for w in list(writes) + list(acc):
            self.last_w[w] = tok
            self.readers[w] = []
        return tok

    def join(self, sem, bufs):
        if self.dry:
            return
        for b in bufs:
            self.last_w[b] = (sem, self.cnt[sem])

    def retire(self, names):
        if self.dry:
            return []
        toks = []
        for n in names:
            if n in self.last_w:
                toks.append(self.last_w.pop(n))
            toks.extend(self.readers.pop(n, ()))
        need = {}
        for (k, v) in toks:
            need[k] = max(need.get(k, 0), v)
        return list(need.items())

    def seed(self, names, toks):
        if self.dry:
            return
        for n in names:
            self.readers.setdefault(n, []).extend(toks)

    def emit(self, final_waits=()):
        nc = self.nc
        with nc.Block() as block:
            for e in ENGS:
                items = self.q[e]
                fw = list(final_waits) if e == "sync" else []

                def body(eh, items=items, fw=fw):
                    for (waits, fn, key, inc) in items:
                        for (k, v) in waits:
                            eh.wait_ge(self.sems[k], v)
                        inst = fn(eh)
                        inst.then_inc(self.sems[key], inc)
                    for (k, v) in fw:
                        eh.wait_ge(self.sems[k], v)

                getattr(block, e)(body)


class WStream:
    def __init__(self, P, slots):
        self.P = P
        self.slots = slots
        self.plan = []
        self.idx = 0
        self.issued = 0

    def start_real(self):
        self.idx = 0
        self.issued = 0

    def get(self, src, n):
        P = self.P
        if P.dry:
            self.plan.append((src, n))
            return self.slots[0], "W0"
        while self.issued < min(len(self.plan), self.idx + NSLOT):
            k = self.issued
            s = k % NSLOT
            sr, nn = self.plan[k]
            P.op("gpsimd", lambda e, s=s, sr=sr, nn=nn: e.dma_start(out=self.slots[s][:, 0:nn], in_=sr),
                 writes=[f"W{s}"], sem=f"dW{s}")
            self.issued += 1
        s = self.idx % NSLOT
        self.idx += 1
        return self.slots[s], f"W{s}"


def build_nc():
    nc = bass.Bass("TRN2", target_bir_lowering=False)

    def din(name, shape, dt=F32):
        return nc.dram_tensor(name, list(shape), dt, kind="ExternalInput").ap()

    xT_d = din("xT", [128, NCH, T])
    cT_d = din("cT", [128, 16])
    wada_d = din("wada", [144, 128, 2048])
    bada_d = din("badaT", [128, 144])
    gains_d = din("gainsT", [128, 64])
    w1in_d = din("w1in", [86, 128, 2048])
    w1out_d = din("w1out", [64, 128, 1408])
    w2in_d = din("w2in", [86, 128, 2048])
    w2out_d = din("w2out", [64, 128, 1408])
    wqkvu_d = din("wqkvu", [32, 128, 2048])
    wf_d = din("wf", [128, 128])
    poolw_d = din("poolw", [4, 128, 512])
    womix_d = din("womix", [16, 128, 2048])
    small_d = din("small", [128, 16])
    wsel_d = din("wsel", [128, 4])
    wown_d = din("wsel_own", [128, 4])
    maskc_d = din("maskc", [128, 512])
    invc_d = din("invc", [128, 512])
    ident_d = din("ident8", [128, 8])
    identf_d = din("identf", [128, 128])
    outT_d = nc.dram_tensor("outT", [128, NCH, T], F32, kind="ExternalOutput").ap()
    dbg_d = None
    if DEBUG:
        dbg_d = nc.dram_tensor("dbg", [4, 128, NCH, T], F32, kind="ExternalOutput").ap()

    agf_in = nc.dram_tensor("agf_in", [136, 1024], F32)
    agf_out = nc.dram_tensor("agf_out", [544, 1024], F32)
    agkv_in = [nc.dram_tensor(f"agkv_in{q}", [32, 16384], BF16) for q in range(4)]
    agkv_out = [nc.dram_tensor(f"agkv_out{q}", [128, 16384], BF16) for q in range(4)]

    with ExitStack() as es:
        P = Prog(nc, es)
        for e in ENGS:
            P.new_sem("E_" + e)
        NW = 52200
        ar = es.enter_context(nc.sbuf_tensor("arena", [128, NW], F32))
        psb = [es.enter_context(nc.psum_tensor(f"psb{i}", [128, 512], F32)) for i in range(8)]

        def fv(off, n):
            return ar[:, off:off + n]

        def bv(off, nw):
            return ar[:, off:off + nw].bitcast(BF16)

        OW = 16384
        OR2 = OW + NSLOT * 1024
        OR3 = OR2 + 8192
        OR4 = OR3 + 8192
        OR5 = OR4 + 4096
        OM = OR5 + 9216
        XT = fv(0, 16384).rearrange("p (c t) -> p c t", t=T)
        wslots = [bv(OW + s * 1024, 1024) for s in range(NSLOT)]
        hT = bv(OR2, 8192).rearrange("p (c t) -> p c t", t=T)
        TL = fv(OR2, 4096).rearrange("p (r x s) -> p r x s", r=4, s=16)
        T1 = fv(OR2 + 4096, 1152).rearrange("p (m s) -> p m s", s=144)
        T2 = fv(OR2 + 5248, 1152).rearrange("p (m s) -> p m s", s=144)
        pooled = bv(OR2 + 6400, 1024).rearrange("p (c t) -> p c t", t=T)
        ptmp = fv(OR2 + 7424, 128)
        Ksl = [bv(OR2 + i * 1024, 1024).rearrange("p (r t) -> p r t", r=4) for i in range(2)]
        Vsl = [bv(OR2 + 2048 + i * 1024, 1024).rearrange("p (r m d) -> p r m d", r=4, m=4) for i in range(2)]
        Fown = fv(OR2 + 4096, 1024)
        negFT = fv(OR2 + 5120, 256)
        Fbc = [fv(OR2 + 5376 + i * 1024, 1024) for i in range(2)]
        gT = bv(OR3, 5632).rearrange("p (c t) -> p c t", t=T)
        sa = [fv(OR3 + 5632 + i * 512, 512) for i in range(2)]
        sqk = [bv(OR3 + i * 256, 256) for i in range(2)]
        rk = [fv(OR3 + 512 + i * 512, 512) for i in range(2)]
        mixT = bv(OR3, 8192).rearrange("p (c t) -> p c t", t=T)
        sq = [bv(OR4 + i * 256, 256) for i in range(2)]
        rstd = fv(OR4 + 512, 1024)
        tt = [fv(OR4 + 1536 + i * 512, 512) for i in range(2)]
        lnv = fv(OR4 + 2560, 512)
        kst = [bv(OR4 + i * 1024, 1024).rearrange("p (h t) -> p h t", h=2) for i in range(2)]
        vst = [bv(OR4 + i * 1024, 256).rearrange("p (m d) -> p m d", m=4) for i in range(2)]
        LFown = fv(OR4 + 2048, 1024)
        qnT = bv(OR4, 4096).rearrange("p (c t) -> p c t", t=T)
        uext = fv(OR5, 9216).rearrange("p (x s) -> p x s", s=144)
        LF = fv(OR5, 4096)
        Fg = fv(OR5 + 4096, 4096)
        ones1k = fv(OR5 + 8192, 1024)
        Sb = [fv(OR5 + i * 512, 512) for i in range(3)]
        PT = [bv(OR5 + 1536 + i * 256, 256) for i in range(4)]
        OLs = [fv(OR5 + 2560 + i * 512, 512) for i in range(2)]
        rl = fv(OR5 + 3584, 512)
        Fm = fv(OR5 + 4096, 1024)
        o = [OM]

        def misc(n):
            a = o[0]
            o[0] += n
            return a
        modT = fv(misc(144), 144)
        bada = fv(misc(144), 144)
        der = fv(misc(144), 144)
        gains = fv(misc(64), 64)
        cT = fv(misc(16), 16)
        cact = bv(misc(8), 8)
        small = fv(misc(16), 16)
        smd = fv(misc(8), 8)
        wsel = fv(misc(4), 4)
        wown = fv(misc(4), 4)
        ident8 = fv(misc(8), 8)
        ones_bf = bv(misc(64), 64)
        ones8 = fv(misc(128), 128)
        maskadd = bv(misc(256), 256).rearrange("p (r t) -> p r t", r=4)
        ident_bf = bv(misc(64), 64)
        invc = fv(misc(512), 512).rearrange("p (g t) -> p g t", g=4)
        wf = bv(misc(64), 64)
        assert o[0] <= NW, o[0]

        for k in ["dW%d" % s for s in range(NSLOT)] + ["dx0", "dx1", "dx2", "dx3", "dconst", "dst0", "dst1",
                                                       "dagf", "dtl", "dlf", "dK0", "dK1", "dV0", "dV1", "dout",
                                                       "ddbg"]:
            P.new_sem(k)
        ws = WStream(P, wslots)

        bank_rr = [0]

        def bank(lim=8):
            i = bank_rr[0] % lim
            bank_rr[0] += 1
            return psb[i], f"ps{i}"

        def hs(half):
            return slice(half * 512, (half + 1) * 512)

        def mm_group(ps_ap, pairs):
            def fn(e):
                inst = None
                n = len(pairs)
                for i, (l, r) in enumerate(pairs):
                    inst = e.matmul(ps_ap, l, r, start=(i == 0), stop=(i == n - 1))
                return inst
            return fn

        def body():
            bank_rr[0] = 0
            cb = []

            def cload(eng, out, in_, name):
                P.op(eng, lambda e: e.dma_start(out=out, in_=in_), writes=[name], sem="dconst")
                cb.append(name)
            cload("sync", cT, cT_d, "cT")
            cload("sync", bada, bada_d, "bada")
            cload("sync", gains, gains_d, "gains")
            cload("sync", small, small_d, "small")
            cload("sync", wsel, wsel_d, "wsel")
            cload("sync", wown, wown_d, "wown")
            cload("sync", invc, invc_d.rearrange("p (g t) -> p g t", g=4), "invc")
            cload("sync", ident8, ident_d, "ident8")
            P.join("dconst", cb)
            P.op("gpsimd", lambda e: e.dma_start(out=wf, in_=wf_d), writes=["wf"], sem="dlf")
            P.op("gpsimd", lambda e: e.dma_start(out=ident_bf, in_=identf_d), writes=["ident_bf"], sem="dlf")
            P.op("gpsimd", lambda e: e.dma_start(out=maskadd, in_=maskc_d.rearrange("p (r t) -> p r t", r=4)),
                 writes=["maskadd"], sem="dlf")
            P.join("dlf", ["wf", "ident_bf", "maskadd"])
            for i in range(4):
                P.op("sync", lambda e, i=i: e.dma_start(out=XT[:, 4 * i:4 * i + 4, :], in_=xT_d[:, 4 * i:4 * i + 4, :]),
                     writes=[f"x{c}_{h}" for c in range(4 * i, 4 * i + 4) for h in range(2)], sem=f"dx{i}")
            P.op("vector", lambda e: e.memset(ones_bf, 1.0), writes=["ones_bf"])
            P.op("vector", lambda e: e.memset(ones8, 1.0), writes=["ones8"])
            P.op("vector", lambda e: e.memset(smd[:, 2:3], EPS), writes=["eps"])
            P.op("vector", lambda e: e.tensor_scalar(smd[:, 0:1], small[:, 0:1], float(128 ** -0.5), None, ALU.mult),
                 reads=["small"], writes=["gqs"])
            P.op("vector", lambda e: e.tensor_scalar(smd[:, 1:2], small[:, 2:3], -1.0, None, ALU.mult),
                 reads=["small"], writes=["negb"])
            P.op("scalar", lambda e: e.activation(cact, cT, AF.Silu), reads=["cT"], writes=["cact"])

            ad_next = [0]

            def adaln_step(n=1):
                for _ in range(n):
                    fc = ad_next[0]
                    if fc >= 144:
                        return
                    ad_next[0] += 1
                    w, wn = ws.get(wada_d[fc], 2048)
                    ps, pn = bank()
                    P.op("tensor", mm_group(ps[:, 0:1], [(w[:, kc * 128:(kc + 1) * 128], cact[:, kc:kc + 1])
                                                        for kc in range(16)]),
                         reads=[wn, "cact"], writes=[pn])
                    P.op("vector", lambda e, ps=ps, fc=fc: e.tensor_tensor(modT[:, fc:fc + 1], ps[:, 0:1],
                                                                         bada[:, fc:fc + 1], ALU.add),
                         reads=[pn, "bada"], writes=[f"mod{fc}"])

            def mod_names(k):
                return [f"mod{k * 16 + c}" for c in range(16)]

            def derive_A(slot, sc_piece, gain_idx, name):
                P.op("vector", lambda e: e.scalar_tensor_tensor(
                    der[:, slot * 16:(slot + 1) * 16], modT[:, sc_piece * 16:(sc_piece + 1) * 16], 1.0,
                    gains[:, gain_idx * 16:(gain_idx + 1) * 16], ALU.add, ALU.mult),
                    reads=mod_names(sc_piece) + ["gains"], writes=[name])

            def derive_HG(slot, g_piece, name):
                P.op("vector", lambda e: e.tensor_scalar(
                    der[:, slot * 16:(slot + 1) * 16], modT[:, g_piece * 16:(g_piece + 1) * 16], 0.5, None, ALU.mult),
                    reads=mod_names(g_piece), writes=[name])

            norm_tmp = ["sq0", "sq1", "rstd0", "rstd1", "tt0", "tt1", "lnv"]

            def rms_stats(scale_dim):
                for half in range(2):
                    ps, pn = bank()
                    for c in range(16):
                        i = c % 2
                        P.op("scalar", lambda e, c=c, i=i, half=half: e.activation(sq[i], XT[:, c, hs(half)], AF.Square),
                             reads=[f"x{c}_{half}"], writes=[f"sq{i}"])
                        P.op("tensor", lambda e, c=c, i=i, ps=ps: e.matmul(ps[:], ones_bf, sq[i], start=(c == 0), stop=(c == 15)),
                             reads=[f"sq{i}", "ones_bf"], acc=[pn])
                    P.op("scalar", lambda e, ps=ps: e.activation(lnv, ps[:], AF.Ln, bias=smd[:, 2:3], scale=1.0 / scale_dim),
                         reads=[pn, "eps"], writes=["lnv"])
                    P.op("scalar", lambda e, half=half: e.activation(rstd[:, hs(half)], lnv, AF.Exp, scale=-0.5),
                         reads=["lnv"], writes=[f"rstd{half}"])

            def norm_mod(Acols, Anames, Bcols, Bnames):
                rms_stats(D)
                for half in range(2):
                    for c in range(16):
                        i = c % 2
                        P.op("vector", lambda e, c=c, i=i, half=half: e.tensor_tensor(tt[i], XT[:, c, hs(half)], rstd[:, hs(half)], ALU.mult),
                             reads=[f"x{c}_{half}", f"rstd{half}"], writes=[f"tt{i}"])
                        P.op("scalar", lambda e, c=c, i=i, half=half: e.activation(
                            hT[:, c, hs(half)], tt[i], AF.Identity, bias=Bcols[:, c:c + 1], scale=Acols[:, c:c + 1]),
                            reads=[f"tt{i}"] + Anames + Bnames, writes=[f"h{c}_{half}"])

            def ffn(win_d, wout_d, HGcols, HGname, hook):
                for G, (f0, nf) in enumerate(GROUPS):
                    for fl in range(nf):
                        f = f0 + fl
                        pss = {}
                        for ab in range(2):
                            w, wn = ws.get(win_d[2 * f + ab], 2048)
                            for half in range(2):
                                ps, pn = bank()
                                pss[(ab, half)] = (ps, pn)
                                P.op("tensor", mm_group(ps[:], [(w[:, kc * 128:(kc + 1) * 128], hT[:, kc, hs(half)])
                                                                for kc in range(16)]),
                                     reads=[wn] + [f"h{kc}_{half}" for kc in range(16)], writes=[pn])
                        for half in range(2):
                            pa, pan = pss[(0, half)]
                            pb, pbn = pss[(1, half)]
                            i = half
                            P.op("scalar", lambda e, pa=pa, i=i: e.activation(sa[i], pa[:], AF.Silu),
                                 reads=[pan], writes=[f"sa{i}"])
                            P.op("vector", lambda e, pb=pb, i=i, fl=fl, half=half: e.tensor_tensor(
                                gT[:, fl, hs(half)], sa[i], pb[:], ALU.mult),
                                reads=[f"sa{i}", pbn], writes=[f"g{fl}_{half}"])
                        hook()
                    if G == 0:
                        hook(final=True)
                    for dc in range(16):
                        w, wn = ws.get(wout_d[G * 16 + dc, :, 0:nf * 128], nf * 128)
                        for half in range(2):
                            ps, pn = bank()
                            P.op("tensor", mm_group(ps[:], [(w[:, fl * 128:(fl + 1) * 128], gT[:, fl, hs(half)])
                                                            for fl in range(nf)]),
                                 reads=[wn] + [f"g{fl}_{half}" for fl in range(nf)], writes=[pn])
                            P.op("vector", lambda e, ps=ps, dc=dc, half=half: e.scalar_tensor_tensor(
                                XT[:, dc, hs(half)], ps[:], HGcols[:, dc:dc + 1], XT[:, dc, hs(half)], ALU.mult, ALU.add),
                                reads=[pn, HGname, f"x{dc}_{half}"], writes=[f"x{dc}_{half}"])
                        hook()

            def dbg_dump(k):
                if DEBUG:
                    P.op("sync", lambda e: e.dma_start(out=dbg_d[k], in_=XT),
                         reads=[f"x{c}_{h}" for c in range(16) for h in range(2)], sem="ddbg")

            def store_out():
                for i in range(4):
                    P.op("sync", lambda e, i=i: e.dma_start(out=outT_d[:, 4 * i:4 * i + 4, :], in_=XT[:, 4 * i:4 * i + 4, :]),
                         reads=[f"x{cc}_{h}" for cc in range(4 * i, 4 * i + 4) for h in range(2)], sem="dout")

            def checkpoint(k):
                if STOP == k:
                    store_out()
                    raise _StopBody()

            checkpoint(0)
            adaln_step(32)
            derive_A(0, 1, 0, "A1")
            norm_mod(der[:, 0:16], ["A1"], modT[:, 0:16], mod_names(0))

            def hook1(final=False):
                if final:
                    while ad_next[0] < 48:
                        adaln_step(1)
                    derive_HG(1, 2, "HG1")
                else:
                    adaln_step(1)
            ffn(w1in_d, w1out_d, der[:, 16:32], "HG1", hook1)
            adaln_step(144)
            dbg_dump(0)
            checkpoint(1)

            derive_A(2, 4, 1, "A2")
            toks = P.retire([f"g{fl}_{h}" for fl in range(11) for h in range(2)] + ["sa0", "sa1"])
            norm_mod(der[:, 32:48], ["A2"], modT[:, 48:64], mod_names(3))
            t4 = P.retire(norm_tmp)
            P.seed(["st0", "st1", "LFown"], t4)
            P.seed(["sqk0", "sqk1", "rk0", "rk1"], toks)
            for half in range(2):
                ps, pn = bank()
                P.op("tensor", mm_group(ps[0:8, :], [(wf[:, kc * 8:(kc + 1) * 8], hT[:, kc, hs(half)]) for kc in range(16)]),
                     reads=["wf"] + [f"h{kc}_{half}" for kc in range(16)], writes=[pn])
                lf = LFown[0:8, hs(half)]
                P.op("scalar", lambda e, ps=ps, lf=lf: e.activation(lf, ps[0:8, :], AF.Exp, bias=smd[0:8, 1:2], scale=-1.0),
                     reads=[pn, "negb"], writes=["LFown"])
                P.op("vector", lambda e, lf=lf: e.tensor_scalar(lf, lf, 1.0, None, ALU.add), reads=["LFown"], writes=["LFown"])
                P.op("scalar", lambda e, lf=lf: e.activation(lf, lf, AF.Ln), reads=["LFown"], writes=["LFown"])
                P.op("vector", lambda e, lf=lf: e.tensor_scalar(lf, lf, -1.0, None, ALU.mult), reads=["LFown"], writes=["LFown"])
            P.op("sync", lambda e: e.dma_start(out=agf_in.ap()[128:136, :], in_=LFown[0:8, :]),
                 reads=["LFown"], writes=["agf_lf"], sem="dagf")
            for c in range(8):
                w, wn = ws.get(wqkvu_d[24 + c], 2048)
                for half in range(2):
                    ps, pn = bank()
                    P.op("tensor", mm_group(ps[:], [(w[:, kc * 128:(kc + 1) * 128], hT[:, kc, hs(half)]) for kc in range(16)]),
                         reads=[wn] + [f"h{kc}_{half}" for kc in range(16)], writes=[pn])
                    P.op("scalar", lambda e, ps=ps, c=c, half=half: e.activation(
                        uext[:, c * 8 + 4 * half:c * 8 + 4 * half + 4, 16:144],
                        ps[:].rearrange("p (a b) -> p a b", b=128), AF.Copy),
                        reads=[pn], writes=[f"u{c}_{half}"])
            unames = [f"u{c}_{h}" for c in range(8) for h in range(2)]
            P.op("sync", lambda e: e.dma_start(out=agf_in.ap()[0:128, :].rearrange("p (x s) -> p x s", s=16),
                                               in_=uext[:, :, 128:144]),
                 reads=unames, writes=["agf_u"], sem="dagf")
            P.op("gpsimd", lambda e: e.collective_compute(
                "AllGather", ALU.bypass, replica_groups=[[0, 1, 2, 3], [4, 5, 6, 7]],
                ins=[agf_in.ap().opt()], outs=[agf_out.ap().opt()]),
                reads=["agf_lf", "agf_u"], writes=["agf_out"])

            checkpoint(2)
            def qk_proj(tile0, gcol, gname, dst_fn, dst_name_fn, after_head):
                pending = []

                def flush():
                    while pending:
                        pending.pop(0)()

                for h in range(8):
                    w, wn = ws.get(wqkvu_d[tile0 + h], 2048)
                    for half in range(2):
                        ps, pn = bank()
                        P.op("tensor", mm_group(ps[:], [(w[:, kc * 128:(kc + 1) * 128], hT[:, kc, hs(half)]) for kc in range(16)]),
                             reads=[wn] + [f"h{kc}_{half}" for kc in range(16)], writes=[pn])
                        flush()

                        def tail(ps=ps, pn=pn, h=h, half=half):
                            i = half
                            P.op("scalar", lambda e: e.activation(sqk[i], ps[:], AF.Square),
                                 reads=[pn], writes=[f"sqk{i}"])
                            ps2, pn2 = bank()
                            P.op("tensor", lambda e: e.matmul(ps2[:], ones_bf, sqk[i], start=True, stop=True),
                                 reads=[f"sqk{i}", "ones_bf"], writes=[pn2])
                            P.op("scalar", lambda e: e.activation(rk[i], ps2[:], AF.Ln, bias=smd[:, 2:3], scale=1.0 / 128),
                                 reads=[pn2, "eps"], writes=[f"rk{i}"])
                            P.op("scalar", lambda e: e.activation(rk[i], rk[i], AF.Exp, scale=-0.5),
                                 reads=[f"rk{i}"], writes=[f"rk{i}"])
                            P.op("vector", lambda e: e.scalar_tensor_tensor(
                                dst_fn(h, half), ps[:], gcol, rk[i], ALU.mult, ALU.mult),
                                reads=[pn, f"rk{i}", gname], writes=[dst_name_fn(h, half)])
                            if half == 1:
                                after_head(h)
                        pending.append(tail)
                flush()

            def k_after(h):
                if h % 2 == 1:
                    s = (h // 2) % 2
                    h0 = h - 1
                    dst = agkv_in[h0 // 4].ap().rearrange("(h a) (b t) -> (a b) h t", a=8, b=16)[:, h0 % 4:h0 % 4 + 2, :]
                    P.op("sync", lambda e, s=s, dst=dst: e.dma_start(out=dst, in_=kst[s]),
                         reads=[f"st{s}"], writes=[f"agk{h0}"], sem=f"dst{s}")
                    if h % 4 == 3:
                        q = h // 4
                        P.op("gpsimd", lambda e, q=q: e.collective_compute(
                            "AllGather", ALU.bypass, replica_groups=[[0, 1, 2, 3], [4, 5, 6, 7]],
                            ins=[agkv_in[q].ap().opt()], outs=[agkv_out[q].ap().opt()]),
                            reads=[f"agk{h - 3}", f"agk{h - 1}"], writes=[f"agkv_out{q}"])
            qk_proj(8, small[:, 1:2], "small", lambda h, half: kst[(h // 2) % 2][:, h % 2, hs(half)],
                    lambda h, half: f"st{(h // 2) % 2}", k_after)
            vi = [0]
            for hv in range(8):
                w, wn = ws.get(wqkvu_d[16 + hv], 2048)
                for mq in range(2):
                    ps, pn = bank()

                    def vfn(e, ps=ps, w=w, mq=mq):
                        inst = None
                        for mm in range(4):
                            m = mq * 4 + mm
                            for kc in range(16):
                                inst = e.matmul(ps[:, mm * 128:(mm + 1) * 128], hT[:, kc, m * 128:(m + 1) * 128],
                                                w[:, kc * 128:(kc + 1) * 128], start=(kc == 0), stop=(kc == 15))
                        return inst
                    P.op("tensor", vfn, reads=[wn] + [f"h{kc}_{mq}" for kc in range(16)], writes=[pn])
                    s = vi[0] % 2
                    vi[0] += 1
                    eng = "scalar" if s == 0 else "vector"
                    if eng == "scalar":
                        P.op("scalar", lambda e, ps=ps, s=s: e.activation(vst[s], ps[:].rearrange("p (m d) -> p m d", d=128), AF.Copy),
                             reads=[pn], writes=[f"st{s}"])
                    else:
                        P.op("vector", lambda e, ps=ps, s=s: e.tensor_copy(vst[s], ps[:].rearrange("p (m d) -> p m d", d=128)),
                             reads=[pn], writes=[f"st{s}"])
                    dst = agkv_in[2 + hv // 4].ap().rearrange("(h m) (i d) -> i h m d", m=8, d=128)[:, hv % 4, mq * 4:(mq + 1) * 4, :]
                    P.op("sync", lambda e, s=s, dst=dst: e.dma_start(out=dst, in_=vst[s]),
                         reads=[f"st{s}"], writes=[f"agv{hv}_{mq}"], sem=f"dst{s}")
                if hv % 4 == 3:
                    q = 2 + hv // 4
                    P.op("gpsimd", lambda e, q=q: e.collective_compute(
                        "AllGather", ALU.bypass, replica_groups=[[0, 1, 2, 3], [4, 5, 6, 7]],
                        ins=[agkv_in[q].ap().opt()], outs=[agkv_out[q].ap().opt()]),
                        reads=[f"agv{hh}_{mq}" for hh in range(hv - 3, hv + 1) for mq in range(2)],
                        writes=[f"agkv_out{q}"])
            t4 = P.retire(["st0", "st1", "LFown"])
            P.seed([f"q{h}_{half}" for h in range(8) for half in range(2)], t4)
            qk_proj(0, smd[:, 0:1], "gqs", lambda h, half: qnT[:, h, hs(half)], lambda h, half: f"q{h}_{half}",
                    lambda h: None)

            checkpoint(3)
            t2 = P.retire([f"h{c}_{h}" for c in range(16) for h in range(2)])
            P.seed(["TL0", "TL1", "TL2", "TL3", "TL3z", "T1", "T2", "pl0", "pl1", "ptmp"], t2)
            t3 = P.retire(["sqk0", "sqk1", "rk0", "rk1"])
            P.seed([f"mx{c}_{h}" for c in range(16) for h in range(2)], t3)
            agf3 = agf_out.ap().rearrange("(r x) c -> r x c", r=4)
            for r in range(3):
                P.op("sync", lambda e, r=r: e.dma_start(out=TL[:, r], in_=agf3[r, 0:128, :].rearrange("p (x s) -> p x s", s=16)),
                     reads=["agf_out"], writes=[f"TL{r}"], sem="dtl")
            TL3v = TL[:, 3].rearrange("p (c m) s -> p c m s", m=8)
            P.op("vector", lambda e: e.memset(TL3v[:, :, 0, :], 0.0), writes=["TL3z"])
            for c in range(8):
                P.op("sync", lambda e, c=c: e.dma_start(
                    out=TL3v[:, c, 1:8, :],
                    in_=agf3[3, 0:128, :].rearrange("p (c m s) -> p c m s", m=8, s=16)[:, c, 0:7, :]),
                    reads=["agf_out"], writes=["TL3"], sem="dtl")
            P.join("dtl", ["TL0", "TL1", "TL2", "TL3"])
            halo = uext[:, :, 0:16]
            P.op("vector", lambda e: e.tensor_scalar(halo, TL[:, 0], wsel[:, 0:1], None, ALU.mult),
                 reads=["TL0", "wsel"], writes=["halo"])
            for r in range(1, 4):
                P.op("vector", lambda e, r=r: e.scalar_tensor_tensor(halo, TL[:, r], wsel[:, r:r + 1], halo, ALU.mult, ALU.add),
                     reads=[f"TL{r}", "TL3z", "wsel", "halo"], writes=["halo"])
            for c in range(8):
                g = c // 2
                wwin = 2 ** (g + 1)
                e_ = uext[:, c * 8:(c + 1) * 8, :]
                cur, curn = e_, None
                bufs = [(T1, "T1"), (T2, "T2")]
                sh = 1
                for lvl in range(g + 1):
                    dst, dn = bufs[lvl % 2]
                    lo = 2 * sh - 1
                    rd = [f"u{c}_0", f"u{c}_1", "halo"] if curn is None else [curn]
                    P.op("vector", lambda e, dst=dst, cur=cur, lo=lo, sh=sh: e.tensor_tensor(
                        dst[:, :, lo:144], cur[:, :, lo:144], cur[:, :, lo - sh:144 - sh], ALU.add),
                        reads=rd, writes=[dn])
                    cur, curn = dst, dn
                    sh *= 2
                cc = c % 2
                P.op("vector", lambda e, cur=cur, e_=e_, cc=cc, wwin=wwin: e.scalar_tensor_tensor(
                    pooled[:, cc, :].rearrange("p (m i) -> p m i", i=128), cur[:, :, 16:144], 1.0 / wwin,
                    e_[:, :, 16:144], ALU.mult, ALU.subtract),
                    reads=[curn, f"u{c}_0", f"u{c}_1"], writes=[f"pl{cc}"])
                P.op("vector", lambda e, cur=cur, g=g: e.tensor_tensor(ptmp, cur[:, 0, 16:144], invc[:, g, :], ALU.mult),
                     reads=[curn, "invc"], writes=["ptmp"])
                P.op("vector", lambda e, e_=e_, cc=cc: e.tensor_tensor(pooled[:, cc, 0:128], ptmp, e_[:, 0, 16:144], ALU.subtract),
                     reads=["ptmp", f"u{c}_0", f"pl{cc}"], writes=[f"pl{cc}"])
                if cc == 1:
                    w, wn = ws.get(poolw_d[g], 512)
                    for dh in range(2):
                        for half in range(2):
                            ps, pn = bank()
                            P.op("tensor", mm_group(ps[:], [(w[:, k2 * 256 + dh * 128:k2 * 256 + dh * 128 + 128],
                                                             pooled[:, k2, hs(half)]) for k2 in range(2)]),
                                 reads=[wn, "pl0", "pl1"], writes=[pn])
                            col = 8 + 2 * g + dh
                            P.op("scalar", lambda e, ps=ps, col=col, half=half: e.activation(
                                mixT[:, col, hs(half)], ps[:], AF.Identity, scale=small[:, col:col + 1]),
                                reads=[pn, "small"], writes=[f"mx{col}_{half}"])

            checkpoint(4)
            t5 = P.retire(unames + ["halo"])
            P.seed(["LF", "Fg", "ones1k"], t5)
            t2 = P.retire(["TL0", "TL1", "TL2", "TL3", "TL3z", "T1", "T2", "pl0", "pl1", "ptmp"])
            P.seed(["K0", "K1", "V0", "V1", "Fown", "negFT", "Fbc0", "Fbc1"], t2)
            LFv = LF[0:8, :].rearrange("p (m r i) -> p m r i", r=4, i=128)
            for r in range(4):
                P.op("sync", lambda e, r=r: e.dma_start(out=LFv[:, :, r, :],
                                                        in_=agf3[r, 128:136, :].rearrange("p (m i) -> p m i", i=128)),
                     reads=["agf_out"], writes=["LF"], sem="dlf")
            P.join("dlf", ["LF", "wf"])
            P.op("vector", lambda e: e.memset(ones1k[0:8, :], 1.0), writes=["ones1k"])
            for qd in range(4):
                init = 0.0 if qd == 0 else Fg[0:8, qd * 1024 - 1:qd * 1024]
                P.op("vector", lambda e, qd=qd, init=init: e.tensor_tensor_scan(
                    Fg[0:8, qd * 1024:(qd + 1) * 1024], ones1k[0:8, :], LF[0:8, qd * 1024:(qd + 1) * 1024],
                    init, ALU.mult, ALU.add),
                    reads=["LF", "ones1k", "Fg"], writes=["Fg"])
            Fgv = Fg[0:8, :].rearrange("p (m r i) -> p m r i", r=4, i=128)
            Fownv = Fown[0:8, :].rearrange("p (m i) -> p m i", i=128)
            P.op("vector", lambda e: e.tensor_scalar(Fownv, Fgv[:, :, 0, :], wown[0:8, 0:1], None, ALU.mult),
                 reads=["Fg", "wown"], writes=["Fown"])
            for r in range(1, 4):
                P.op("vector", lambda e, r=r: e.scalar_tensor_tensor(Fownv, Fgv[:, :, r, :], wown[0:8, r:r + 1], Fownv,
                                                                    ALU.mult, ALU.add),
                     reads=["Fg", "wown", "Fown"], writes=["Fown"])
            ps, pn = bank()

            def trf(e, ps=ps):
                inst = None
                for jk in range(32):
                    inst = e.transpose(ps[:, jk * 8:(jk + 1) * 8], Fg[0:8, jk * 128:(jk + 1) * 128], ident8[0:8, 0:8])
                return inst
            P.op("tensor", trf, reads=["Fg", "ident8"], writes=[pn])
            P.op("vector", lambda e, ps=ps: e.tensor_scalar(negFT, ps[:, 0:256], -1.0, None, ALU.mult),
                 reads=[pn], writes=["negFT"])

            checkpoint(5)
            t5 = P.retire(["LF", "Fg", "ones1k"])
            P.seed(["Sb0", "Sb1", "Sb2", "PT0", "PT1", "PT2", "PT3", "OL0", "OL1", "rl", "Fm"], t5)
            Ksrcs = [agkv_out[q].ap().rearrange("(r x) c -> r x c", r=4).rearrange("r (h a) (b t) -> h (a b) r t", a=8, b=16)
                     for q in range(2)]
            Vsrcs = [agkv_out[2 + q].ap().rearrange("(r x) c -> r x c", r=4).rearrange("r (h m) (i d) -> h i r m d", m=8, d=128)
                     for q in range(2)]
            psO = [(psb[4], "ps4"), (psb[5], "ps5")]
            psL = [(psb[6], "ps6"), (psb[7], "ps7")]
            LA = 2
            NSB, NPT = 3, 4
            chunks = []
            for h in range(8):
                for mh in range(2):
                    for mm in range(4):
                        m = mh * 4 + mm
                        for r in range(4):
                            jk = 4 * m + r
                            c0 = m * 128
                            cl = [(c0, 512, 0), (512, 1024, 1)] if c0 < 512 else [(c0, 1024, 1)]
                            for ci, (a, b, bk) in enumerate(cl):
                                chunks.append(dict(h=h, mh=mh, mm=mm, r=r, jk=jk, a=a, b=b, bk=bk, c0=c0,
                                                   lastc=(ci == len(cl) - 1), idx=len(chunks)))
            seen_h = set()
            seen_seg = set()

            def stage_a(ch):
                h, mh, mm, r, jk, a, b, c0 = ch["h"], ch["mh"], ch["mm"], ch["r"], ch["jk"], ch["a"], ch["b"], ch["c0"]
                fb = h % 2
                s = (2 * h + mh) % 2
                if h not in seen_h:
                    seen_h.add(h)
                    P.op("vector", lambda e, h=h: e.tensor_scalar(Fm[0:8, :], Fown[0:8, :], ident8[0:8, h:h + 1], None, ALU.mult),
                         reads=["Fown", "ident8"], writes=["Fm"])
                    for half in range(2):
                        ps, pn = bank(4)
                        P.op("tensor", lambda e, ps=ps, half=half: e.matmul(ps[:], ones8[0:8, :], Fm[0:8, hs(half)], start=True, stop=True),
                             reads=["Fm", "ones8"], writes=[pn])
                        P.op("scalar", lambda e, ps=ps, half=half, fb=fb: e.activation(Fbc[fb][:, hs(half)], ps[:], AF.Copy),
                             reads=[pn], writes=[f"Fbc{fb}"])
                if (h, mh) not in seen_seg:
                    seen_seg.add((h, mh))
                    P.op("sync", lambda e, s=s, h=h, mh=mh: e.dma_start(out=Ksl[s], in_=Ksrcs[h // 4][h % 4][:, :, mh * 512:(mh + 1) * 512]),
                         reads=[f"agkv_out{h // 4}"], writes=[f"K{s}"], sem=f"dK{s}")
                    for rr in range(4):
                        P.op("sync", lambda e, s=s, h=h, mh=mh, rr=rr: e.dma_start(
                            out=Vsl[s][:, rr], in_=Vsrcs[h // 4][h % 4][:, rr, mh * 4:(mh + 1) * 4, :]),
                            reads=[f"agkv_out{2 + h // 4}"], writes=[f"V{s}"], sem=f"dV{s}")
                    P.join(f"dV{s}", [f"V{s}"])
                n = b - a
                ps, pn = bank(4)
                diag = (a == c0)

                def sfn(e, ps=ps, s=s, r=r, mm=mm, h=h, a=a, b=b, n=n, diag=diag):
                    inst = e.matmul(ps[:, 0:n], Ksl[s][:, r, mm * 128:(mm + 1) * 128], qnT[:, h, a:b],
                                    start=True, stop=(not diag))
                    if diag:
                        inst = e.matmul(ps[:, 0:128], ident_bf, maskadd[:, r, :], start=False, stop=True)
                    return inst
                P.op("tensor", sfn, reads=[f"K{s}", f"q{h}_0", f"q{h}_1", "ident_bf", "maskadd"], writes=[pn])
                si = ch["idx"] % NSB
                P.op("vector", lambda e, ps=ps, si=si, n=n, fb=fb, a=a, b=b: e.tensor_tensor(
                    Sb[si][:, 0:n], ps[:, 0:n], Fbc[fb][:, a:b], ALU.add),
                    reads=[pn, f"Fbc{fb}"], writes=[f"Sb{si}"])
                pi = ch["idx"] % NPT
                P.op("scalar", lambda e, si=si, pi=pi, n=n, jk=jk, h=h: e.activation(
                    PT[pi][:, 0:n], Sb[si][:, 0:n], AF.Exp, bias=negFT[:, jk * 8 + h:jk * 8 + h + 1]),
                    reads=[f"Sb{si}", "negFT"], writes=[f"PT{pi}"])

            def stage_b(ch):
                h, mh, mm, r, jk, a, b, bk = ch["h"], ch["mh"], ch["mm"], ch["r"], ch["jk"], ch["a"], ch["b"], ch["bk"]
                s = (2 * h + mh) % 2
                n = b - a
                pi = ch["idx"] % NPT
                lastjk = 15 if bk == 0 else 31
                oa, ob = a - 512 * bk, b - 512 * bk

                def pvf(e, pi=pi, n=n, s=s, r=r, mm=mm, bk=bk, oa=oa, ob=ob, jk=jk, lastjk=lastjk):
                    e.matmul(psO[bk][0][:, oa:ob], Vsl[s][:, r, mm, :], PT[pi][:, 0:n],
                             start=(jk == 0), stop=(jk == lastjk))
                    return e.matmul(psL[bk][0][:, oa:ob], ones_bf, PT[pi][:, 0:n],
                                    start=(jk == 0), stop=(jk == lastjk))
                P.op("tensor", pvf, reads=[f"V{s}", f"PT{pi}", "ones_bf"], acc=[psO[bk][1], psL[bk][1]])
                if ch["lastc"]:
                    for bk2 in range(2):
                        if jk == (15 if bk2 == 0 else 31):
                            P.op("scalar", lambda e, bk2=bk2: e.activation(OLs[0], psO[bk2][0][:], AF.Copy),
                                 reads=[psO[bk2][1]], writes=["OL0"])
                            P.op("scalar", lambda e, bk2=bk2: e.activation(OLs[1], psL[bk2][0][:], AF.Copy),
                                 reads=[psL[bk2][1]], writes=["OL1"])
                            P.op("vector", lambda e: e.reciprocal(rl, OLs[1]), reads=["OL1"], writes=["rl"])
                            P.op("vector", lambda e, h=h, bk2=bk2: e.tensor_tensor(mixT[:, h, hs(bk2)], OLs[0], rl, ALU.mult),
                                 reads=["OL0", "rl"], writes=[f"mx{h}_{bk2}"])

            nch = len(chunks)
            for i in range(nch + LA):
                if i < nch:
                    stage_a(chunks[i])
                if i - LA >= 0:
                    stage_b(chunks[i - LA])

            checkpoint(6)
            for dc in range(16):
                w, wn = ws.get(womix_d[dc], 2048)
                for half in range(2):
                    ps, pn = bank()
                    P.op("tensor", mm_group(ps[:], [(w[:, kc * 128:(kc + 1) * 128], mixT[:, kc, hs(half)]) for kc in range(16)]),
                         reads=[wn] + [f"mx{kc}_{half}" for kc in range(16)], writes=[pn])
                    P.op("vector", lambda e, ps=ps, dc=dc, half=half: e.scalar_tensor_tensor(
                        XT[:, dc, hs(half)], ps[:], modT[:, 80 + dc:81 + dc], XT[:, dc, hs(half)], ALU.mult, ALU.add),
                        reads=[pn, f"mod{80 + dc}", f"x{dc}_{half}"], writes=[f"x{dc}_{half}"])
            dbg_dump(1)
            checkpoint(7)

            derive_A(3, 7, 2, "A3")
            derive_HG(4, 8, "HG3")
            t4 = P.retire([f"q{h}_{half}" for h in range(8) for half in range(2)])
            P.seed(norm_tmp, t4)
            t2 = P.retire(["K0", "K1", "V0", "V1", "Fown", "negFT", "Fbc0", "Fbc1"])
            P.seed([f"h{c}_{h}" for c in range(16) for h in range(2)], t2)
            t3 = P.retire([f"mx{c}_{h}" for c in range(16) for h in range(2)])
            P.seed([f"g{fl}_{h}" for fl in range(11) for h in range(2)] + ["sa0", "sa1"], t3)
            norm_mod(der[:, 48:64], ["A3"], modT[:, 96:112], mod_names(6))
            ffn(w2in_d, w2out_d, der[:, 64:80], "HG3", lambda final=False: None)
            dbg_dump(2)

            rms_stats(D)
            for c in range(16):
                for half in range(2):
                    P.op("vector", lambda e, c=c, half=half: e.scalar_tensor_tensor(
                        XT[:, c, hs(half)], XT[:, c, hs(half)], gains[:, 48 + c:49 + c], rstd[:, hs(half)], ALU.mult, ALU.mult),
                        reads=[f"x{c}_{half}", f"rstd{half}", "gains"], writes=[f"x{c}_{half}"])
                if c % 4 == 3:
                    i = c // 4
                    P.op("sync", lambda e, i=i: e.dma_start(out=outT_d[:, 4 * i:4 * i + 4, :], in_=XT[:, 4 * i:4 * i + 4, :]),
                         reads=[f"x{cc}_{h}" for cc in range(4 * i, 4 * i + 4) for h in range(2)], sem="dout")

        P.dry = True
        try:
            body()
        except _StopBody:
            pass
        P.dry = False
        P.reset()
        ws.start_real()
        try:
            body()
        except _StopBody:
            pass
        fw = [("dout", P.cnt["dout"])]
        if DEBUG:
            fw.append(("ddbg", P.cnt["ddbg"]))
        P.emit(final_waits=fw)
    return nc


_NC_CACHE = {}


def _tile_kn(w, col0, ncols):
    K = w.shape[0]
    sub = w[:, col0:col0 + ncols]
    nt = ncols // 128
    a = sub.reshape(K // 128, 128, nt, 128)
    return np.ascontiguousarray(a.transpose(2, 1, 0, 3)).reshape(nt, 128, (K // 128) * 128)


def _prep_shared(inp):
    f32 = np.float32
    sh = {}
    w_ada = np.asarray(inp["w_ada"], f32)[0]
    sh["wada"] = _tile_kn(w_ada, 0, 9 * D)
    sh["badaT"] = np.ascontiguousarray(np.asarray(inp["b_ada"], f32)[0].reshape(144, 128).T)
    gains = np.stack([np.asarray(inp["ffn1_norm_g"], f32)[0], np.asarray(inp["mix_norm_g"], f32)[0],
                      np.asarray(inp["ffn2_norm_g"], f32)[0], np.asarray(inp["final_norm_g"], f32)], 0)
    sh["gainsT"] = np.ascontiguousarray(gains.reshape(4, 16, 128).transpose(2, 0, 1).reshape(128, 64))
    for nm, key_in, key_out in (("1", "ffn1_w_in", "ffn1_w_out"), ("2", "ffn2_w_in", "ffn2_w_out")):
        win = np.asarray(inp[key_in], f32)[0]
        ta = _tile_kn(win, 0, DFF)
        tb = _tile_kn(win, DFF, DFF)
        sh[f"w{nm}in"] = np.ascontiguousarray(np.stack([ta, tb], 1).reshape(86, 128, 2048))
        wout = np.asarray(inp[key_out], f32)[0]
        tiles = np.zeros((64, 128, 1408), f32)
        for G, (f0, nf) in enumerate(GROUPS):
            blk = wout[f0 * 128:(f0 + nf) * 128].reshape(nf, 128, 16, 128)
            tiles[G * 16:(G + 1) * 16, :, 0:nf * 128] = blk.transpose(2, 1, 0, 3).reshape(16, 128, nf * 128)
        sh[f"w{nm}out"] = tiles
    w_in = np.asarray(inp["w_in"], f32)[0]
    sh["wqkvu"] = np.ascontiguousarray(np.concatenate(
        [_tile_kn(w_in, 0, 1024), _tile_kn(w_in, 1024, 1024), _tile_kn(w_in, 2048, 1024), _tile_kn(w_in, 3080, 1024)], 0))
    wfc = w_in[:, 3072:3080].reshape(16, 128, 8)
    sh["wf"] = np.ascontiguousarray(wfc.transpose(1, 0, 2).reshape(128, 128))
    pw = np.asarray(inp["pool_w"], f32)[0]
    sh["poolw"] = np.ascontiguousarray(pw.reshape(4, 2, 128, 256).transpose(0, 2, 1, 3).reshape(4, 128, 512))
    sh["womix"] = _tile_kn(np.asarray(inp["w_out"], f32)[0], 0, D)
    small = np.zeros((128, 16), f32)
    small[:, 0] = np.asarray(inp["q_norm_g"], f32)[0]
    small[:, 1] = np.asarray(inp["k_norm_g"], f32)[0]
    small[0:8, 2] = np.asarray(inp["b_forget"], f32)[0]
    small[:, 8:16] = np.asarray(inp["pool_scale"], f32)[0].reshape(8, 128).T
    sh["small"] = small
    sh["ident8"] = np.ascontiguousarray(np.eye(128, 8, dtype=f32))
    sh["identf"] = np.ascontiguousarray(np.eye(128, dtype=f32))
    return sh


def _prep_core(inp, core):
    f32 = np.float32
    b, j = core // 4, core % 4
    x = np.asarray(inp["x"], f32)[b]
    xs = x.reshape(8, 4, 128, D)[:, j].reshape(T, D)
    d = {}
    d["xT"] = np.ascontiguousarray(xs.T.reshape(16, 128, T).transpose(1, 0, 2))
    d["cT"] = np.ascontiguousarray(np.asarray(inp["c"], f32)[b].reshape(16, 128).T)
    wsel = np.zeros((128, 4), f32)
    wsel[:, (j - 1) % 4] = 1.0
    d["wsel_halo"] = wsel
    mask = np.zeros((128, 4, 128), f32)
    for r in range(4):
        if r < j:
            mask[:, r, :] = 0.0
        elif r == j:
            s_ = np.arange(128)[:, None]
            t_ = np.arange(128)[None, :]
            mask[:, r, :] = np.where(s_ <= t_, 0.0, -30000.0)
        else:
            mask[:, r, :] = -30000.0
    d["maskc"] = mask.reshape(128, 512)
    invc = np.zeros((128, 4, 128), f32)
    for g in range(4):
        w = 2 ** (g + 1)
        if j == 0:
            invc[:, g, :] = 1.0 / np.minimum(np.arange(128) + 1, w)
        else:
            invc[:, g, :] = 1.0 / w
    d["invc"] = invc.reshape(128, 512)
    return d


def kernel(**inp):
    if "nc" not in _NC_CACHE:
        _NC_CACHE["nc"] = build_nc()
    nc = _NC_CACHE["nc"]
    sh = _prep_shared(inp)
    in_maps = []
    for core in range(8):
        d = _prep_core(inp, core)
        j = core % 4
        m = dict(sh)
        m["xT"] = d["xT"]
        m["cT"] = d["cT"]
        m["maskc"] = d["maskc"]
        m["invc"] = d["invc"]
        m["wsel"] = d["wsel_halo"]
        ws2 = np.zeros((128, 4), np.float32)
        ws2[:, j] = 1.0
        m["wsel_own"] = ws2
        in_maps.append(m)
    res = run_bass_kernel_spmd(nc, in_maps, core_ids=list(range(8)))
    out = np.empty((2, 4096, D), np.float32)
    for core in range(8):
        b, j = core // 4, core % 4
        oT = res.results[core]["outT"]
        loc = oT.transpose(2, 1, 0).reshape(T, D)
        out[b].reshape(8, 4, 128, D)[:, j] = loc.reshape(8, 128, D)
    if DEBUG:
        kernel.last_dbg = [res.results[c]["dbg"] for c in range(8)]
    return out
```

```python
import numpy as np
import concourse.bass as bass
import concourse.mybir as mybir
from concourse.bass_utils import run_bass_kernel_spmd
from contextlib import ExitStack

F32 = mybir.dt.float32
BF16 = mybir.dt.bfloat16
AF = mybir.ActivationFunctionType
ALU = mybir.AluOpType

ENGS = ["tensor", "vector", "scalar", "gpsimd", "sync"]
D = 2048
T = 1024
NCH = 16
DFF = 5504
NF = 43
GROUPS = [(0, 11), (11, 11), (22, 11), (33, 10)]
NSLOT = 4
EPS = 1e-6
DEBUG = False
STOP = 99


class _StopBody(Exception):
    pass


class Prog:
    def __init__(self, nc, es):
        self.nc = nc
        self.es = es
        self.dry = False
        self.sems = {}
        self.reset()

    def reset(self):
        self.q = {e: [] for e in ENGS}
        self.cnt = {k: 0 for k in self.sems}
        self.waited = {}
        self.last_w = {}
        self.readers = {}

    def new_sem(self, key):
        if key not in self.sems:
            self.sems[key] = self.es.enter_context(self.nc.semaphore(key))
        self.cnt[key] = 0
        return key

    def op(self, eng, fn, reads=(), writes=(), acc=(), deps=(), sem=None):
        if self.dry:
            return None
        mykey = "E_" + eng
        alld = list(deps)
        for r in reads:
            if r in self.last_w:
                alld.append(self.last_w[r])
        for w in writes:
            if w in self.last_w:
                alld.append(self.last_w[w])
            alld.extend(self.readers.get(w, ()))
        for w in acc:
            if w in self.last_w and self.last_w[w][0] != mykey:
                alld.append(self.last_w[w])
            alld.extend(self.readers.get(w, ()))
        need = {}
        for (k, v) in alld:
            need[k] = max(need.get(k, 0), v)
        waits = []
        for k, v in need.items():
            if v > self.waited.get((eng, k), 0):
                self.waited[(eng, k)] = v
                waits.append((k, v))
        if sem is None:
            key, inc = mykey, 1
        else:
            key, inc = sem, 16
        self.cnt[key] += inc
        tok = (key, self.cnt[key])
        self.q[eng].append((waits, fn, key, inc))
        for r in reads:
            self.readers.setdefault(r, []).append(tok)
        for w in list(writes) + list(acc):
            self.last_w[w] = tok
            self.readers[w] = []
        return tok

    def join(self, sem, bufs):
        if self.dry:
            return
        for b in bufs:
            self.last_w[b] = (sem, self.cnt[sem])

    def retire(self, names):
        if self.dry:
            return []
        toks = []
        for n in names:
            if n in self.last_w:
                toks.append(self.last_w.pop(n))
            toks.extend(self.readers.pop(n, ()))
        need = {}
        for (k, v) in toks:
            need[k] = max(need.get(k, 0), v)
        return list(need.items())

    def seed(self, names, toks):
        if self.dry:
            return
        for n in names:
            self.readers.setdefault(n, []).extend(toks)

    def emit(self, final_waits=()):
        nc = self.nc
        with nc.Block() as block:
            for e in ENGS:
                items = self.q[e]
                fw = list(final_waits) if e == "sync" else []

                def body(eh, items=items, fw=fw):
                    for (waits, fn, key, inc) in items:
                        for (k, v) in waits:
                            eh.wait_ge(self.sems[k], v)
                        inst = fn(eh)
                        inst.then_inc(self.sems[key], inc)
                    for (k, v) in fw:
                        eh.wait_ge(self.sems[k], v)

                getattr(block, e)(body)


class WStream:
    def __init__(self, P, slots):
        self.P = P
        self.slots = slots
        self.plan = []
        self.idx = 0
        self.issued = 0

    def start_real(self):
        self.idx = 0
        self.issued = 0

    def get(self, src, n):
        P = self.P
        if P.dry:
            self.plan.append((src, n))
            return self.slots[0], "W0"
        while self.issued < min(len(self.plan), self.idx + NSLOT):
            k = self.issued
            s = k % NSLOT
            sr, nn = self.plan[k]
            P.op("gpsimd", lambda e, s=s, sr=sr, nn=nn: e.dma_start(out=self.slots[s][:, 0:nn], in_=sr),
                 writes=[f"W{s}"], sem=f"dW{s}")
            self.issued += 1
        s = self.idx % NSLOT
        self.idx += 1
        return self.slots[s], f"W{s}"


def build_nc():
    nc = bass.Bass("TRN2", target_bir_lowering=False)

    def din(name, shape, dt=F32):
        return nc.dram_tensor(name, list(shape), dt, kind="ExternalInput").ap()

    xT_d = din("xT", [128, NCH, T])
    cT_d = din("cT", [128, 32])
    wada_d = din("wada", [18, 128, 2048])
    bada_d = din("badaT", [128, 18])
    wb_d = din("wb", [128, 2])
    gains_d = din("gainsT", [128, 64])
    w1in_d = din("w1in", [86, 128, 2048])
    w1out_d = din("w1out", [64, 128, 1408])
    w2in_d = din("w2in", [86, 128, 2048])
    w2out_d = din("w2out", [64, 128, 1408])
    wqkvu_d = din("wqkvu", [32, 128, 2048])
    wf_d = din("wf", [128, 128])
    poolw_d = din("poolw", [4, 128, 512])
    womix_d = din("womix", [16, 128, 2048])
    small_d = din("small", [128, 16])
    wsel_d = din("wsel", [128, 4])
    wown_d = din("wsel_own", [128, 4])
    maskc_d = din("maskc", [128, 512])
    invc_d = din("invc", [128, 512])
    ident_d = din("ident8", [128, 8])
    identf_d = din("identf", [128, 128])
    outT_d = nc.dram_tensor("outT", [128, NCH, T], F32, kind="ExternalOutput").ap()
    dbg_d = None
    if DEBUG:
        dbg_d = nc.dram_tensor("dbg", [4, 128, NCH, T], F32, kind="ExternalOutput").ap()

    agm_in = nc.dram_tensor("agm_in", [128, 36], F32)
    agm_out = nc.dram_tensor("agm_out", [1024, 36], F32)
    agf_in = nc.dram_tensor("agf_in", [136, 1024], F32)
    agf_out = nc.dram_tensor("agf_out", [544, 1024], F32)
    agkv_in = [nc.dram_tensor(f"agkv_in{q}", [32, 16384], BF16) for q in range(4)]
    agkv_out = [nc.dram_tensor(f"agkv_out{q}", [128, 16384], BF16) for q in range(4)]

    with ExitStack() as es:
        P = Prog(nc, es)
        for e in ENGS:
            P.new_sem("E_" + e)
        NW = 53000
        ar = es.enter_context(nc.sbuf_tensor("arena", [128, NW], F32))
        psb = [es.enter_context(nc.psum_tensor(f"psb{i}", [128, 512], F32)) for i in range(8)]

        def fv(off, n):
            return ar[:, off:off + n]

        def bv(off, nw):
            return ar[:, off:off + nw].bitcast(BF16)

        OW = 16384
        OR2 = OW + NSLOT * 1024
        OR3 = OR2 + 8192
        OR4 = OR3 + 8192
        OR5 = OR4 + 4096
        OM = OR5 + 9216
        XT = fv(0, 16384).rearrange("p (c t) -> p c t", t=T)
        wslots = [bv(OW + s * 1024, 1024) for s in range(NSLOT)]
        hT = bv(OR2, 8192).rearrange("p (c t) -> p c t", t=T)
        TL = fv(OR2, 4096).rearrange("p (r x s) -> p r x s", r=4, s=16)
        T1 = fv(OR2 + 4096, 1152).rearrange("p (m s) -> p m s", s=144)
        T2 = fv(OR2 + 5248, 1152).rearrange("p (m s) -> p m s", s=144)
        pooled = bv(OR2 + 6400, 1024).rearrange("p (c t) -> p c t", t=T)
        ptmp = fv(OR2 + 7424, 128)
        Ksl = [bv(OR2 + i * 1024, 1024).rearrange("p (r t) -> p r t", r=4) for i in range(2)]
        Vsl = [bv(OR2 + 2048 + i * 1024, 1024).rearrange("p (r m d) -> p r m d", r=4, m=4) for i in range(2)]
        Fown = fv(OR2 + 4096, 1024)
        negFT = fv(OR2 + 5120, 256)
        Fbc = [fv(OR2 + 5376 + i * 1024, 1024) for i in range(2)]
        gT = bv(OR3, 5632).rearrange("p (c t) -> p c t", t=T)
        sa = [fv(OR3 + 5632 + i * 512, 512) for i in range(2)]
        sqk = [bv(OR3 + i * 256, 256) for i in range(2)]
        rk = [fv(OR3 + 512 + i * 512, 512) for i in range(2)]
        mixT = bv(OR3, 8192).rearrange("p (c t) -> p c t", t=T)
        sq = [bv(OR4 + i * 256, 256) for i in range(2)]
        rstd = fv(OR4 + 512, 1024)
        tt = [fv(OR4 + 1536 + i * 512, 512) for i in range(2)]
        lnv = fv(OR4 + 2560, 512)
        kst = [bv(OR4 + i * 1024, 1024).rearrange("p (h t) -> p h t", h=2) for i in range(2)]
        vst = [bv(OR4 + i * 1024, 256).rearrange("p (m d) -> p m d", m=4) for i in range(2)]
        LFown = fv(OR4 + 2048, 1024)
        qnT = bv(OR4, 4096).rearrange("p (c t) -> p c t", t=T)
        uext = fv(OR5, 9216).rearrange("p (x s) -> p x s", s=144)
        LF = fv(OR5, 4096)
        Fg = fv(OR5 + 4096, 4096)
        ones1k = fv(OR5 + 8192, 1024)
        Sb = [fv(OR5 + i * 512, 512) for i in range(3)]
        PT = [bv(OR5 + 1536 + i * 256, 256) for i in range(4)]
        OLs = [fv(OR5 + 2560 + i * 512, 512) for i in range(2)]
        rl = fv(OR5 + 3584, 512)
        Fm = fv(OR5 + 4096, 1024)
        o = [OM]

        def misc(n):
            a = o[0]
            o[0] += n
            return a
        modT = fv(misc(144), 144)
        bada = fv(misc(144), 144)
        der = fv(misc(144), 144)
        gains = fv(misc(64), 64)
        cT = fv(misc(32), 32)
        cact = bv(misc(16), 16)
        mloc = fv(misc(36), 36)
        Mall = fv(misc(288), 288)
        wb = fv(misc(2), 2)
        small = fv(misc(16), 16)
        smd = fv(misc(8), 8)
        wsel = fv(misc(4), 4)
        wown = fv(misc(4), 4)
        ident8 = fv(misc(8), 8)
        ones_bf = bv(misc(64), 64)
        ones8 = fv(misc(128), 128)
        maskadd = bv(misc(256), 256).rearrange("p (r t) -> p r t", r=4)
        ident_bf = bv(misc(64), 64)
        invc = fv(misc(512), 512).rearrange("p (g t) -> p g t", g=4)
        wf = bv(misc(64), 64)
        assert o[0] <= NW, o[0]

        for k in ["dW%d" % s for s in range(NSLOT)] + ["dx0", "dx1", "dx2", "dx3", "dconst", "dst0", "dst1",
                                                       "dagf", "dagm", "dtl", "dlf", "dK0", "dK1", "dV0", "dV1", "dout",
                                                       "ddbg"]:
            P.new_sem(k)
        ws = WStream(P, wslots)

        bank_rr = [0]

        def bank(lim=8):
            i = bank_rr[0] % lim
            bank_rr[0] += 1
            return psb[i], f"ps{i}"

        def hs(half):
            return slice(half * 512, (half + 1) * 512)

        def mm_group(ps_ap, pairs):
            def fn(e):
                inst = None
                n = len(pairs)
                for i, (l, r) in enumerate(pairs):
                    inst = e.matmul(ps_ap, l, r, start=(i == 0), stop=(i == n - 1))
                return inst
            return fn

        def body():
            bank_rr[0] = 0
            cb = []

            def cload(eng, out, in_, name):
                P.op(eng, lambda e: e.dma_start(out=out, in_=in_), writes=[name], sem="dconst")
                cb.append(name)
            cload("sync", cT, cT_d, "cT")
            cload("sync", bada[:, 0:18], bada_d, "bada")
            cload("sync", wb, wb_d, "wb")
            cload("sync", gains, gains_d, "gains")
            cload("sync", small, small_d, "small")
            cload("sync", wsel, wsel_d, "wsel")
            cload("sync", wown, wown_d, "wown")
            cload("sync", invc, invc_d.rearrange("p (g t) -> p g t", g=4), "invc")
            cload("sync", ident8, ident_d, "ident8")
            P.join("dconst", cb)
            P.op("gpsimd", lambda e: e.dma_start(out=wf, in_=wf_d), writes=["wf"], sem="dlf")
            P.op("gpsimd", lambda e: e.dma_start(out=ident_bf, in_=identf_d), writes=["ident_bf"], sem="dlf")
            P.op("gpsimd", lambda e: e.dma_start(out=maskadd, in_=maskc_d.rearrange("p (r t) -> p r t", r=4)),
                 writes=["maskadd"], sem="dlf")
            P.join("dlf", ["wf", "ident_bf", "maskadd"])
            for i in range(4):
                P.op("sync", lambda e, i=i: e.dma_start(out=XT[:, 4 * i:4 * i + 4, :], in_=xT_d[:, 4 * i:4 * i + 4, :]),
                     writes=[f"x{c}_{h}" for c in range(4 * i, 4 * i + 4) for h in range(2)], sem=f"dx{i}")
            P.op("vector", lambda e: e.memset(ones_bf, 1.0), writes=["ones_bf"])
            P.op("vector", lambda e: e.memset(ones8, 1.0), writes=["ones8"])
            P.op("vector", lambda e: e.memset(smd[:, 2:3], EPS), writes=["eps"])
            P.op("vector", lambda e: e.tensor_scalar(smd[:, 0:1], small[:, 0:1], float(128 ** -0.5), None, ALU.mult),
                 reads=["small"], writes=["gqs"])
            P.op("vector", lambda e: e.tensor_scalar(smd[:, 1:2], small[:, 2:3], -1.0, None, ALU.mult),
                 reads=["small"], writes=["negb"])
            P.op("scalar", lambda e: e.activation(cact, cT, AF.Silu), reads=["cT"], writes=["cact"])

            def adaln_all():
                for i in range(18):
                    w, wn = ws.get(wada_d[i], 2048)
                    ps, pn = bank()
                    P.op("tensor", mm_group(ps[:, 0:2], [(w[:, kc * 128:(kc + 1) * 128], cact[:, 2 * kc:2 * kc + 2])
                                                        for kc in range(16)]),
                         reads=[wn, "cact"], writes=[pn])
                    P.op("vector", lambda e, ps=ps, i=i: e.tensor_scalar(mloc[:, 2 * i:2 * i + 2], ps[:, 0:2],
                                                                       bada[:, i:i + 1], None, ALU.add),
                         reads=[pn, "bada"], writes=["mloc"])
                P.op("sync", lambda e: e.dma_start(out=agm_in.ap(), in_=mloc), reads=["mloc"], writes=["agm_in"], sem="dagm")
                P.op("gpsimd", lambda e: e.collective_compute(
                    "AllGather", ALU.bypass, replica_groups=[[0, 1, 2, 3, 4, 5, 6, 7]],
                    ins=[agm_in.ap().opt()], outs=[agm_out.ap().opt()]),
                    reads=["agm_in"], writes=["agm_out"])
                P.op("sync", lambda e: e.dma_start(out=Mall.rearrange("p (r x) -> p r x", r=8),
                                                   in_=agm_out.ap().rearrange("(r p) x -> p r x", r=8)),
                     reads=["agm_out"], writes=["Mall"], sem="dagm")
                Mv = Mall.rearrange("p (y b) -> p y b", b=2)
                P.op("vector", lambda e: e.tensor_scalar(modT, Mv[:, :, 0], wb[:, 0:1], None, ALU.mult),
                     reads=["Mall", "wb"], writes=["modT"])
                P.op("vector", lambda e: e.scalar_tensor_tensor(modT, Mv[:, :, 1], wb[:, 1:2], modT, ALU.mult, ALU.add),
                     reads=["Mall", "wb", "modT"], writes=["modT"])

            def mod_names(k):
                return ["modT"]

            def derive_A(slot, sc_piece, gain_idx, name):
                P.op("vector", lambda e: e.scalar_tensor_tensor(
                    der[:, slot * 16:(slot + 1) * 16], modT[:, sc_piece * 16:(sc_piece + 1) * 16], 1.0,
                    gains[:, gain_idx * 16:(gain_idx + 1) * 16], ALU.add, ALU.mult),
                    reads=mod_names(sc_piece) + ["gains"], writes=[name])

            def derive_HG(slot, g_piece, name):
                P.op("vector", lambda e: e.tensor_scalar(
                    der[:, slot * 16:(slot + 1) * 16], modT[:, g_piece * 16:(g_piece + 1) * 16], 0.5, None, ALU.mult),
                    reads=mod_names(g_piece), writes=[name])

            norm_tmp = ["sq0", "sq1", "rstd0", "rstd1", "tt0", "tt1", "lnv"]

            def rms_stats(scale_dim):
                for half in range(2):
                    ps, pn = bank()
                    for c in range(16):
                        i = c % 2
                        P.op("scalar", lambda e, c=c, i=i, half=half: e.activation(sq[i], XT[:, c, hs(half)], AF.Square),
                             reads=[f"x{c}_{half}"], writes=[f"sq{i}"])
                        P.op("tensor", lambda e, c=c, i=i, ps=ps: e.matmul(ps[:], ones_bf, sq[i], start=(c == 0), stop=(c == 15)),
                             reads=[f"sq{i}", "ones_bf"], acc=[pn])
                    P.op("scalar", lambda e, ps=ps: e.activation(lnv, ps[:], AF.Ln, bias=smd[:, 2:3], scale=1.0 / scale_dim),
                         reads=[pn, "eps"], writes=["lnv"])
                    P.op("scalar", lambda e, half=half: e.activation(rstd[:, hs(half)], lnv, AF.Exp, scale=-0.5),
                         reads=["lnv"], writes=[f"rstd{half}"])

            def norm_mod(Acols, Anames, Bcols, Bnames, stats=True):
                if stats:
                    rms_stats(D)
                for half in range(2):
                    for c in range(16):
                        i = c % 2
                        P.op("vector", lambda e, c=c, i=i, half=half: e.tensor_tensor(tt[i], XT[:, c, hs(half)], rstd[:, hs(half)], ALU.mult),
                             reads=[f"x{c}_{half}", f"rstd{half}"], writes=[f"tt{i}"])
                        P.op("scalar", lambda e, c=c, i=i, half=half: e.activation(
                            hT[:, c, hs(half)], tt[i], AF.Identity, bias=Bcols[:, c:c + 1], scale=Acols[:, c:c + 1]),
                            reads=[f"tt{i}"] + Anames + Bnames, writes=[f"h{c}_{half}"])

            def ffn(win_d, wout_d, HGcols, HGname, hook):
                for G, (f0, nf) in enumerate(GROUPS):
                    for fl in range(nf):
                        f = f0 + fl
                        pss = {}
                        for ab in range(2):
                            w, wn = ws.get(win_d[2 * f + ab], 2048)
                            for half in range(2):
                                ps, pn = bank()
                                pss[(ab, half)] = (ps, pn)
                                P.op("tensor", mm_group(ps[:], [(w[:, kc * 128:(kc + 1) * 128], hT[:, kc, hs(half)])
                                                                for kc in range(16)]),
                                     reads=[wn] + [f"h{kc}_{half}" for kc in range(16)], writes=[pn])
                        for half in range(2):
                            pa, pan = pss[(0, half)]
                            pb, pbn = pss[(1, half)]
                            i = half
                            P.op("scalar", lambda e, pa=pa, i=i: e.activation(sa[i], pa[:], AF.Silu),
                                 reads=[pan], writes=[f"sa{i}"])
                            P.op("vector", lambda e, pb=pb, i=i, fl=fl, half=half: e.tensor_tensor(
                                gT[:, fl, hs(half)], sa[i], pb[:], ALU.mult),
                                reads=[f"sa{i}", pbn], writes=[f"g{fl}_{half}"])
                        hook()
                    if G == 0:
                        hook(final=True)
                    for dc in range(16):
                        w, wn = ws.get(wout_d[G * 16 + dc, :, 0:nf * 128], nf * 128)
                        for half in range(2):
                            ps, pn = bank()
                            P.op("tensor", mm_group(ps[:], [(w[:, fl * 128:(fl + 1) * 128], gT[:, fl, hs(half)])
                                                            for fl in range(nf)]),
                                 reads=[wn] + [f"g{fl}_{half}" for fl in range(nf)], writes=[pn])
                            P.op("vector", lambda e, ps=ps, dc=dc, half=half: e.scalar_tensor_tensor(
                                XT[:, dc, hs(half)], ps[:], HGcols[:, dc:dc + 1], XT[:, dc, hs(half)], ALU.mult, ALU.add),
                                reads=[pn, HGname, f"x{dc}_{half}"], writes=[f"x{dc}_{half}"])
                        hook()

            def dbg_dump(k):
                if DEBUG:
                    P.op("sync", lambda e: e.dma_start(out=dbg_d[k], in_=XT),
                         reads=[f"x{c}_{h}" for c in range(16) for h in range(2)], sem="ddbg")

            def store_out():
                for i in range(4):
                    P.op("sync", lambda e, i=i: e.dma_start(out=outT_d[:, 4 * i:4 * i + 4, :], in_=XT[:, 4 * i:4 * i + 4, :]),
                         reads=[f"x{cc}_{h}" for cc in range(4 * i, 4 * i + 4) for h in range(2)], sem="dout")

            def checkpoint(k):
                if STOP == k:
                    store_out()
                    raise _StopBody()

            checkpoint(0)
            rms_stats(D)
            adaln_all()
            derive_A(0, 1, 0, "A1")
            derive_HG(1, 2, "HG1")
            norm_mod(der[:, 0:16], ["A1"], modT[:, 0:16], mod_names(0), stats=False)
            ffn(w1in_d, w1out_d, der[:, 16:32], "HG1", lambda final=False: None)
            dbg_dump(0)
            checkpoint(1)

            ag_pending = []
            ag_state = {"tiles": 0, "last": -100}

            def ag_submit(ins_t, outs_t, reads, writes):
                def issue():
                    P.op("gpsimd", lambda e: e.collective_compute(
                        "AllGather", ALU.bypass, replica_groups=[[0, 1, 2, 3], [4, 5, 6, 7]],
                        ins=[ins_t.ap().opt()], outs=[outs_t.ap().opt()]),
                        reads=reads, writes=writes)
                ag_pending.append(issue)

            def ag_pump(force=False):
                ag_state["tiles"] += 0 if force else 1
                while ag_pending and (force or ag_state["tiles"] - ag_state["last"] >= 6):
                    ag_pending.pop(0)()
                    ag_state["last"] = ag_state["tiles"]
                    if not force:
                        break
            derive_A(2, 4, 1, "A2")
            toks = P.retire([f"g{fl}_{h}" for fl in range(11) for h in range(2)] + ["sa0", "sa1"])
            norm_mod(der[:, 32:48], ["A2"], modT[:, 48:64], mod_names(3))
            t4 = P.retire(norm_tmp)
            P.seed(["st0", "st1", "LFown"], t4)
            P.seed(["sqk0", "sqk1", "rk0", "rk1"], toks)
            for half in range(2):
                ps, pn = bank()
                P.op("tensor", mm_group(ps[0:8, :], [(wf[:, kc * 8:(kc + 1) * 8], hT[:, kc, hs(half)]) for kc in range(16)]),
                     reads=["wf"] + [f"h{kc}_{half}" for kc in range(16)], writes=[pn])
                lf = LFown[0:8, hs(half)]
                P.op("scalar", lambda e, ps=ps, lf=lf: e.activation(lf, ps[0:8, :], AF.Exp, bias=smd[0:8, 1:2], scale=-1.0),
                     reads=[pn, "negb"], writes=["LFown"])
                P.op("vector", lambda e, lf=lf: e.tensor_scalar(lf, lf, 1.0, None, ALU.add), reads=["LFown"], writes=["LFown"])
                P.op("scalar", lambda e, lf=lf: e.activation(lf, lf, AF.Ln), reads=["LFown"], writes=["LFown"])
                P.op("vector", lambda e, lf=lf: e.tensor_scalar(lf, lf, -1.0, None, ALU.mult), reads=["LFown"], writes=["LFown"])
            P.op("sync", lambda e: e.dma_start(out=agf_in.ap()[128:136, :], in_=LFown[0:8, :]),
                 reads=["LFown"], writes=["agf_lf"], sem="dagf")
            for c in range(8):
                w, wn = ws.get(wqkvu_d[24 + c], 2048)
                for half in range(2):
                    ps, pn = bank()
                    P.op("tensor", mm_group(ps[:], [(w[:, kc * 128:(kc + 1) * 128], hT[:, kc, hs(half)]) for kc in range(16)]),
                         reads=[wn] + [f"h{kc}_{half}" for kc in range(16)], writes=[pn])
                    P.op("scalar", lambda e, ps=ps, c=c, half=half: e.activation(
                        uext[:, c * 8 + 4 * half:c * 8 + 4 * half + 4, 16:144],
                        ps[:].rearrange("p (a b) -> p a b", b=128), AF.Copy),
                        reads=[pn], writes=[f"u{c}_{half}"])
            unames = [f"u{c}_{h}" for c in range(8) for h in range(2)]
            P.op("sync", lambda e: e.dma_start(out=agf_in.ap()[0:128, :].rearrange("p (x s) -> p x s", s=16),
                                               in_=uext[:, :, 128:144]),
                 reads=unames, writes=["agf_u"], sem="dagf")
            ag_submit(agf_in, agf_out, ["agf_lf", "agf_u"], ["agf_out"])
            ag_state["tiles"] = 8
            ag_pump()

            checkpoint(2)
            def qk_proj(tile0, gcol, gname, dst_fn, dst_name_fn, after_head):
                pending = []

                def flush():
                    while pending:
                        pending.pop(0)()

                for h in range(8):
                    w, wn = ws.get(wqkvu_d[tile0 + h], 2048)
                    for half in range(2):
                        ps, pn = bank()
                        P.op("tensor", mm_group(ps[:], [(w[:, kc * 128:(kc + 1) * 128], hT[:, kc, hs(half)]) for kc in range(16)]),
                             reads=[wn] + [f"h{kc}_{half}" for kc in range(16)], writes=[pn])
                        flush()

                        def tail(ps=ps, pn=pn, h=h, half=half):
                            i = half
                            P.op("scalar", lambda e: e.activation(sqk[i], ps[:], AF.Square),
                                 reads=[pn], writes=[f"sqk{i}"])
                            ps2, pn2 = bank()
                            P.op("tensor", lambda e: e.matmul(ps2[:], ones_bf, sqk[i], start=True, stop=True),
                                 reads=[f"sqk{i}", "ones_bf"], writes=[pn2])
                            P.op("scalar", lambda e: e.activation(rk[i], ps2[:], AF.Ln, bias=smd[:, 2:3], scale=1.0 / 128),
                                 reads=[pn2, "eps"], writes=[f"rk{i}"])
                            P.op("scalar", lambda e: e.activation(rk[i], rk[i], AF.Exp, scale=-0.5),
                                 reads=[f"rk{i}"], writes=[f"rk{i}"])
                            P.op("vector", lambda e: e.scalar_tensor_tensor(
                                dst_fn(h, half), ps[:], gcol, rk[i], ALU.mult, ALU.mult),
                                reads=[pn, f"rk{i}", gname], writes=[dst_name_fn(h, half)])
                            if half == 1:
                                after_head(h)
                                ag_pump()
                        pending.append(tail)
                flush()

            def k_after(h):
                if h % 2 == 1:
                    s = (h // 2) % 2
                    h0 = h - 1
                    dst = agkv_in[h0 // 4].ap().rearrange("(h a) (b t) -> (a b) h t", a=8, b=16)[:, h0 % 4:h0 % 4 + 2, :]
                    P.op("sync", lambda e, s=s, dst=dst: e.dma_start(out=dst, in_=kst[s]),
                         reads=[f"st{s}"], writes=[f"agk{h0}"], sem=f"dst{s}")
                    if h % 4 == 3:
                        q = h // 4
                        ag_submit(agkv_in[q], agkv_out[q], [f"agk{h - 3}", f"agk{h - 1}"], [f"agkv_out{q}"])
            qk_proj(8, small[:, 1:2], "small", lambda h, half: kst[(h // 2) % 2][:, h % 2, hs(half)],
                    lambda h, half: f"st{(h // 2) % 2}", k_after)
            vi = [0]
            for hv in range(8):
                w, wn = ws.get(wqkvu_d[16 + hv], 2048)
                for mq in range(2):
                    ps, pn = bank()

                    def vfn(e, ps=ps, w=w, mq=mq):
                        inst = None
                        for mm in range(4):
                            m = mq * 4 + mm
                            for kc in range(16):
                                inst = e.matmul(ps[:, mm * 128:(mm + 1) * 128], hT[:, kc, m * 128:(m + 1) * 128],
                                                w[:, kc * 128:(kc + 1) * 128], start=(kc == 0), stop=(kc == 15))
                        return inst
                    P.op("tensor", vfn, reads=[wn] + [f"h{kc}_{mq}" for kc in range(16)], writes=[pn])
                    s = vi[0] % 2
                    vi[0] += 1
                    eng = "scalar" if s == 0 else "vector"
                    if eng == "scalar":
                        P.op("scalar", lambda e, ps=ps, s=s: e.activation(vst[s], ps[:].rearrange("p (m d) -> p m d", d=128), AF.Copy),
                             reads=[pn], writes=[f"st{s}"])
                    else:
                        P.op("vector", lambda e, ps=ps, s=s: e.tensor_copy(vst[s], ps[:].rearrange("p (m d) -> p m d", d=128)),
                             reads=[pn], writes=[f"st{s}"])
                    dst = agkv_in[2 + hv // 4].ap().rearrange("(h m) (i d) -> i h m d", m=8, d=128)[:, hv % 4, mq * 4:(mq + 1) * 4, :]
                    P.op("sync", lambda e, s=s, dst=dst: e.dma_start(out=dst, in_=vst[s]),
                         reads=[f"st{s}"], writes=[f"agv{hv}_{mq}"], sem=f"dst{s}")
                if hv % 4 == 3:
                    q = 2 + hv // 4
                    ag_submit(agkv_in[q], agkv_out[q],
                              [f"agv{hh}_{mq}" for hh in range(hv - 3, hv + 1) for mq in range(2)], [f"agkv_out{q}"])
                ag_pump()
            t4 = P.retire(["st0", "st1", "LFown"])
            P.seed([f"q{h}_{half}" for h in range(8) for half in range(2)], t4)
            qk_proj(0, smd[:, 0:1], "gqs", lambda h, half: qnT[:, h, hs(half)], lambda h, half: f"q{h}_{half}",
                    lambda h: None)

            ag_pump(force=True)
            checkpoint(3)
            t2 = P.retire([f"h{c}_{h}" for c in range(16) for h in range(2)])
            P.seed(["TL0", "TL1", "TL2", "TL3", "TL3z", "T1", "T2", "pl0", "pl1", "ptmp"], t2)
            t3 = P.retire(["sqk0", "sqk1", "rk0", "rk1"])
            P.seed([f"mx{c}_{h}" for c in range(16) for h in range(2)], t3)
            agf3 = agf_out.ap().rearrange("(r x) c -> r x c", r=4)
            for r in range(3):
                P.op("sync", lambda e, r=r: e.dma_start(out=TL[:, r], in_=agf3[r, 0:128, :].rearrange("p (x s) -> p x s", s=16)),
                     reads=["agf_out"], writes=[f"TL{r}"], sem="dtl")
            TL3v = TL[:, 3].rearrange("p (c m) s -> p c m s", m=8)
            P.op("vector", lambda e: e.memset(TL3v[:, :, 0, :], 0.0), writes=["TL3z"])
            for c in range(8):
                P.op("sync", lambda e, c=c: e.dma_start(
                    out=TL3v[:, c, 1:8, :],
                    in_=agf3[3, 0:128, :].rearrange("p (c m s) -> p c m s", m=8, s=16)[:, c, 0:7, :]),
                    reads=["agf_out"], writes=["TL3"], sem="dtl")
            P.join("dtl", ["TL0", "TL1", "TL2", "TL3"])
            halo = uext[:, :, 0:16]
            P.op("vector", lambda e: e.tensor_scalar(halo, TL[:, 0], wsel[:, 0:1], None, ALU.mult),
                 reads=["TL0", "wsel"], writes=["halo"])
            for r in range(1, 4):
                P.op("vector", lambda e, r=r: e.scalar_tensor_tensor(halo, TL[:, r], wsel[:, r:r + 1], halo, ALU.mult, ALU.add),
                     reads=[f"TL{r}", "TL3z", "wsel", "halo"], writes=["halo"])
            for c in range(8):
                g = c // 2
                wwin = 2 ** (g + 1)
                e_ = uext[:, c * 8:(c + 1) * 8, :]
                cur, curn = e_, None
                bufs = [(T1, "T1"), (T2, "T2")]
                sh = 1
                for lvl in range(g + 1):
                    dst, dn = bufs[lvl % 2]
                    lo = 2 * sh - 1
                    rd = [f"u{c}_0", f"u{c}_1", "halo"] if curn is None else [curn]
                    P.op("vector", lambda e, dst=dst, cur=cur, lo=lo, sh=sh: e.tensor_tensor(
                        dst[:, :, lo:144], cur[:, :, lo:144], cur[:, :, lo - sh:144 - sh], ALU.add),
                        reads=rd, writes=[dn])
                    cur, curn = dst, dn
                    sh *= 2
                cc = c % 2
                P.op("vector", lambda e, cur=cur, e_=e_, cc=cc, wwin=wwin: e.scalar_tensor_tensor(
                    pooled[:, cc, :].rearrange("p (m i) -> p m i", i=128), cur[:, :, 16:144], 1.0 / wwin,
                    e_[:, :, 16:144], ALU.mult, ALU.subtract),
                    reads=[curn, f"u{c}_0", f"u{c}_1"], writes=[f"pl{cc}"])
                P.op("vector", lambda e, cur=cur, g=g: e.tensor_tensor(ptmp, cur[:, 0, 16:144], invc[:, g, :], ALU.mult),
                     reads=[curn, "invc"], writes=["ptmp"])
                P.op("vector", lambda e, e_=e_, cc=cc: e.tensor_tensor(pooled[:, cc, 0:128], ptmp, e_[:, 0, 16:144], ALU.subtract),
                     reads=["ptmp", f"u{c}_0", f"pl{cc}"], writes=[f"pl{cc}"])
                if cc == 1:
                    w, wn = ws.get(poolw_d[g], 512)
                    for dh in range(2):
                        for half in range(2):
                            ps, pn = bank()
                            P.op("tensor", mm_group(ps[:], [(w[:, k2 * 256 + dh * 128:k2 * 256 + dh * 128 + 128],
                                                             pooled[:, k2, hs(half)]) for k2 in range(2)]),
                                 reads=[wn, "pl0", "pl1"], writes=[pn])
                            col = 8 + 2 * g + dh
                            P.op("scalar", lambda e, ps=ps, col=col, half=half: e.activation(
                                mixT[:, col, hs(half)], ps[:], AF.Identity, scale=small[:, col:col + 1]),
                                reads=[pn, "small"], writes=[f"mx{col}_{half}"])

            checkpoint(4)
            t5 = P.retire(unames + ["halo"])
            P.seed(["LF", "Fg", "ones1k"], t5)
            t2 = P.retire(["TL0", "TL1", "TL2", "TL3", "TL3z", "T1", "T2", "pl0", "pl1", "ptmp"])
            P.seed(["K0", "K1", "V0", "V1", "Fown", "negFT", "Fbc0", "Fbc1"], t2)
            LFv = LF[0:8, :].rearrange("p (m r i) -> p m r i", r=4, i=128)
            for r in range(4):
                P.op("sync", lambda e, r=r: e.dma_start(out=LFv[:, :, r, :],
                                                        in_=agf3[r, 128:136, :].rearrange("p (m i) -> p m i", i=128)),
                     reads=["agf_out"], writes=["LF"], sem="dlf")
            P.join("dlf", ["LF", "wf"])
            P.op("vector", lambda e: e.memset(ones1k[0:8, :], 1.0), writes=["ones1k"])
            for qd in range(4):
                init = 0.0 if qd == 0 else Fg[0:8, qd * 1024 - 1:qd * 1024]
                P.op("vector", lambda e, qd=qd, init=init: e.tensor_tensor_scan(
                    Fg[0:8, qd * 1024:(qd + 1) * 1024], ones1k[0:8, :], LF[0:8, qd * 1024:(qd + 1) * 1024],
                    init, ALU.mult, ALU.add),
                    reads=["LF", "ones1k", "Fg"], writes=["Fg"])
            Fgv = Fg[0:8, :].rearrange("p (m r i) -> p m r i", r=4, i=128)
            Fownv = Fown[0:8, :].rearrange("p (m i) -> p m i", i=128)
            P.op("vector", lambda e: e.tensor_scalar(Fownv, Fgv[:, :, 0, :], wown[0:8, 0:1], None, ALU.mult),
                 reads=["Fg", "wown"], writes=["Fown"])
            for r in range(1, 4):
                P.op("vector", lambda e, r=r: e.scalar_tensor_tensor(Fownv, Fgv[:, :, r, :], wown[0:8, r:r + 1], Fownv,
                                                                    ALU.mult, ALU.add),
                     reads=["Fg", "wown", "Fown"], writes=["Fown"])
            ps, pn = bank()

            def trf(e, ps=ps):
                inst = None
                for jk in range(32):
                    inst = e.transpose(ps[:, jk * 8:(jk + 1) * 8], Fg[0:8, jk * 128:(jk + 1) * 128], ident8[0:8, 0:8])
                return inst
            P.op("tensor", trf, reads=["Fg", "ident8"], writes=[pn])
            P.op("vector", lambda e, ps=ps: e.tensor_scalar(negFT, ps[:, 0:256], -1.0, None, ALU.mult),
                 reads=[pn], writes=["negFT"])

            checkpoint(5)
            t5 = P.retire(["LF", "Fg", "ones1k"])
            P.seed(["Sb0", "Sb1", "Sb2", "PT0", "PT1", "PT2", "PT3", "OL0", "OL1", "rl", "Fm"], t5)
            Ksrcs = [agkv_out[q].ap().rearrange("(r x) c -> r x c", r=4).rearrange("r (h a) (b t) -> h (a b) r t", a=8, b=16)
                     for q in range(2)]
            Vsrcs = [agkv_out[2 + q].ap().rearrange("(r x) c -> r x c", r=4).rearrange("r (h m) (i d) -> h i r m d", m=8, d=128)
                     for q in range(2)]
            psO = [(psb[4], "ps4"), (psb[5], "ps5")]
            psL = [(psb[6], "ps6"), (psb[7], "ps7")]
            LA = 2
            NSB, NPT = 3, 4
            chunks = []
            for h in range(8):
                for mh in range(2):
                    for mm in range(4):
                        m = mh * 4 + mm
                        for r in range(4):
                            jk = 4 * m + r
                            c0 = m * 128
                            cl = [(c0, 512, 0), (512, 1024, 1)] if c0 < 512 else [(c0, 1024, 1)]
                            for ci, (a, b, bk) in enumerate(cl):
                                chunks.append(dict(h=h, mh=mh, mm=mm, r=r, jk=jk, a=a, b=b, bk=bk, c0=c0,
                                                   lastc=(ci == len(cl) - 1), idx=len(chunks)))
            seen_h = set()
            seen_seg = set()

            def stage_a(ch):
                h, mh, mm, r, jk, a, b, c0 = ch["h"], ch["mh"], ch["mm"], ch["r"], ch["jk"], ch["a"], ch["b"], ch["c0"]
                fb = h % 2
                s = (2 * h + mh) % 2
                if h not in seen_h:
                    seen_h.add(h)
                    P.op("vector", lambda e, h=h: e.tensor_scalar(Fm[0:8, :], Fown[0:8, :], ident8[0:8, h:h + 1], None, ALU.mult),
                         reads=["Fown", "ident8"], writes=["Fm"])
                    for half in range(2):
                        ps, pn = bank(4)
                        P.op("tensor", lambda e, ps=ps, half=half: e.matmul(ps[:], ones8[0:8, :], Fm[0:8, hs(half)], start=True, stop=True),
                             reads=["Fm", "ones8"], writes=[pn])
                        P.op("scalar", lambda e, ps=ps, half=half, fb=fb: e.activation(Fbc[fb][:, hs(half)], ps[:], AF.Copy),
                             reads=[pn], writes=[f"Fbc{fb}"])
                if (h, mh) not in seen_seg:
                    seen_seg.add((h, mh))
                    P.op("sync", lambda e, s=s, h=h, mh=mh: e.dma_start(out=Ksl[s], in_=Ksrcs[h // 4][h % 4][:, :, mh * 512:(mh + 1) * 512]),
                         reads=[f"agkv_out{h // 4}"], writes=[f"K{s}"], sem=f"dK{s}")
                    for rr in range(4):
                        P.op("sync", lambda e, s=s, h=h, mh=mh, rr=rr: e.dma_start(
                            out=Vsl[s][:, rr], in_=Vsrcs[h // 4][h % 4][:, rr, mh * 4:(mh + 1) * 4, :]),
                            reads=[f"agkv_out{2 + h // 4}"], writes=[f"V{s}"], sem=f"dV{s}")
                    P.join(f"dV{s}", [f"V{s}"])
                n = b - a
                ps, pn = bank(4)
                diag = (a == c0)

                def sfn(e, ps=ps, s=s, r=r, mm=mm, h=h, a=a, b=b, n=n, diag=diag):
                    inst = e.matmul(ps[:, 0:n], Ksl[s][:, r, mm * 128:(mm + 1) * 128], qnT[:, h, a:b],
                                    start=True, stop=(not diag))
                    if diag:
                        inst = e.matmul(ps[:, 0:128], ident_bf, maskadd[:, r, :], start=False, stop=True)
                    return inst
                P.op("tensor", sfn, reads=[f"K{s}", f"q{h}_0", f"q{h}_1", "ident_bf", "maskadd"], writes=[pn])
                si = ch["idx"] % NSB
                P.op("vector", lambda e, ps=ps, si=si, n=n, fb=fb, a=a, b=b: e.tensor_tensor(
                    Sb[si][:, 0:n], ps[:, 0:n], Fbc[fb][:, a:b], ALU.add),
                    reads=[pn, f"Fbc{fb}"], writes=[f"Sb{si}"])
                pi = ch["idx"] % NPT
                P.op("scalar", lambda e, si=si, pi=pi, n=n, jk=jk, h=h: e.activation(
                    PT[pi][:, 0:n], Sb[si][:, 0:n], AF.Exp, bias=negFT[:, jk * 8 + h:jk * 8 + h + 1]),
                    reads=[f"Sb{si}", "negFT"], writes=[f"PT{pi}"])

            def stage_b(ch):
                h, mh, mm, r, jk, a, b, bk = ch["h"], ch["mh"], ch["mm"], ch["r"], ch["jk"], ch["a"], ch["b"], ch["bk"]
                s = (2 * h + mh) % 2
                n = b - a
                pi = ch["idx"] % NPT
                lastjk = 15 if bk == 0 else 31
                oa, ob = a - 512 * bk, b - 512 * bk

                def pvf(e, pi=pi, n=n, s=s, r=r, mm=mm, bk=bk, oa=oa, ob=ob, jk=jk, lastjk=lastjk):
                    e.matmul(psO[bk][0][:, oa:ob], Vsl[s][:, r, mm, :], PT[pi][:, 0:n],
                             start=(jk == 0), stop=(jk == lastjk))
                    return e.matmul(psL[bk][0][:, oa:ob], ones_bf, PT[pi][:, 0:n],
                                    start=(jk == 0), stop=(jk == lastjk))
                P.op("tensor", pvf, reads=[f"V{s}", f"PT{pi}", "ones_bf"], acc=[psO[bk][1], psL[bk][1]])
                if ch["lastc"]:
                    for bk2 in range(2):
                        if jk == (15 if bk2 == 0 else 31):
                            P.op("scalar", lambda e, bk2=bk2: e.activation(OLs[0], psO[bk2][0][:], AF.Copy),
                                 reads=[psO[bk2][1]], writes=["OL0"])
                            P.op("scalar", lambda e, bk2=bk2: e.activation(OLs[1], psL[bk2][0][:], AF.Copy),
                                 reads=[psL[bk2][1]], writes=["OL1"])
                            P.op("vector", lambda e: e.reciprocal(rl, OLs[1]), reads=["OL1"], writes=["rl"])
                            P.op("vector", lambda e, h=h, bk2=bk2: e.tensor_tensor(mixT[:, h, hs(bk2)], OLs[0], rl, ALU.mult),
                                 reads=["OL0", "rl"], writes=[f"mx{h}_{bk2}"])

            nch = len(chunks)
            for i in range(nch + LA):
                if i < nch:
                    stage_a(chunks[i])
                if i - LA >= 0:
                    stage_b(chunks[i - LA])

            checkpoint(6)
            for dc in range(16):
                w, wn = ws.get(womix_d[dc], 2048)
                for half in range(2):
                    ps, pn = bank()
                    P.op("tensor", mm_group(ps[:], [(w[:, kc * 128:(kc + 1) * 128], mixT[:, kc, hs(half)]) for kc in range(16)]),
                         reads=[wn] + [f"mx{kc}_{half}" for kc in range(16)], writes=[pn])
                    P.op("vector", lambda e, ps=ps, dc=dc, half=half: e.scalar_tensor_tensor(
                        XT[:, dc, hs(half)], ps[:], modT[:, 80 + dc:81 + dc], XT[:, dc, hs(half)], ALU.mult, ALU.add),
                        reads=[pn, "modT", f"x{dc}_{half}"], writes=[f"x{dc}_{half}"])
            dbg_dump(1)
            checkpoint(7)

            derive_A(3, 7, 2, "A3")
            derive_HG(4, 8, "HG3")
            t4 = P.retire([f"q{h}_{half}" for h in range(8) for half in range(2)])
            P.seed(norm_tmp, t4)
            t2 = P.retire(["K0", "K1", "V0", "V1", "Fown", "negFT", "Fbc0", "Fbc1"])
            P.seed([f"h{c}_{h}" for c in range(16) for h in range(2)], t2)
            t3 = P.retire([f"mx{c}_{h}" for c in range(16) for h in range(2)])
            P.seed([f"g{fl}_{h}" for fl in range(11) for h in range(2)] + ["sa0", "sa1"], t3)
            norm_mod(der[:, 48:64], ["A3"], modT[:, 96:112], mod_names(6))
            ffn(w2in_d, w2out_d, der[:, 64:80], "HG3", lambda final=False: None)
            dbg_dump(2)

            rms_stats(D)
            for c in range(16):
                for half in range(2):
                    P.op("vector", lambda e, c=c, half=half: e.scalar_tensor_tensor(
                        XT[:, c, hs(half)], XT[:, c, hs(half)], gains[:, 48 + c:49 + c], rstd[:, hs(half)], ALU.mult, ALU.mult),
                        reads=[f"x{c}_{half}", f"rstd{half}", "gains"], writes=[f"x{c}_{half}"])
                if c % 4 == 3:
                    i = c // 4
                    P.op("sync", lambda e, i=i: e.dma_start(out=outT_d[:, 4 * i:4 * i + 4, :], in_=XT[:, 4 * i:4 * i + 4, :]),
                         reads=[f"x{cc}_{h}" for cc in range(4 * i, 4 * i + 4) for h in range(2)], sem="dout")

        P.dry = True
        try:
            body()
        except _StopBody:
            pass
        P.dry = False
        P.reset()
        ws.start_real()
        try:
            body()
        except _StopBody:
            pass
        fw = [("dout", P.cnt["dout"])]
        if DEBUG:
            fw.append(("ddbg", P.cnt["ddbg"]))
        P.emit(final_waits=fw)
    return nc


_NC_CACHE = {}


def _tile_kn(w, col0, ncols):
    K = w.shape[0]
    sub = w[:, col0:col0 + ncols]
    nt = ncols // 128
    a = sub.reshape(K // 128, 128, nt, 128)
    return np.ascontiguousarray(a.transpose(2, 1, 0, 3)).reshape(nt, 128, (K // 128) * 128)


def _prep_shared(inp):
    f32 = np.float32
    sh = {}
    w_ada = np.asarray(inp["w_ada"], f32)[0]
    sh["wada"] = _tile_kn(w_ada, 0, 9 * D)
    sh["badaT_full"] = np.ascontiguousarray(np.asarray(inp["b_ada"], f32)[0].reshape(144, 128).T)
    gains = np.stack([np.asarray(inp["ffn1_norm_g"], f32)[0], np.asarray(inp["mix_norm_g"], f32)[0],
                      np.asarray(inp["ffn2_norm_g"], f32)[0], np.asarray(inp["final_norm_g"], f32)], 0)
    sh["gainsT"] = np.ascontiguousarray(gains.reshape(4, 16, 128).transpose(2, 0, 1).reshape(128, 64))
    for nm, key_in, key_out in (("1", "ffn1_w_in", "ffn1_w_out"), ("2", "ffn2_w_in", "ffn2_w_out")):
        win = np.asarray(inp[key_in], f32)[0]
        ta = _tile_kn(win, 0, DFF)
        tb = _tile_kn(win, DFF, DFF)
        sh[f"w{nm}in"] = np.ascontiguousarray(np.stack([ta, tb], 1).reshape(86, 128, 2048))
        wout = np.asarray(inp[key_out], f32)[0]
        tiles = np.zeros((64, 128, 1408), f32)
        for G, (f0, nf) in enumerate(GROUPS):
            blk = wout[f0 * 128:(f0 + nf) * 128].reshape(nf, 128, 16, 128)
            tiles[G * 16:(G + 1) * 16, :, 0:nf * 128] = blk.transpose(2, 1, 0, 3).reshape(16, 128, nf * 128)
        sh[f"w{nm}out"] = tiles
    w_in = np.asarray(inp["w_in"], f32)[0]
    sh["wqkvu"] = np.ascontiguousarray(np.concatenate(
        [_tile_kn(w_in, 0, 1024), _tile_kn(w_in, 1024, 1024), _tile_kn(w_in, 2048, 1024), _tile_kn(w_in, 3080, 1024)], 0))
    wfc = w_in[:, 3072:3080].reshape(16, 128, 8)
    sh["wf"] = np.ascontiguousarray(wfc.transpose(1, 0, 2).reshape(128, 128))
    pw = np.asarray(inp["pool_w"], f32)[0]
    sh["poolw"] = np.ascontiguousarray(pw.reshape(4, 2, 128, 256).transpose(0, 2, 1, 3).reshape(4, 128, 512))
    sh["womix"] = _tile_kn(np.asarray(inp["w_out"], f32)[0], 0, D)
    small = np.zeros((128, 16), f32)
    small[:, 0] = np.asarray(inp["q_norm_g"], f32)[0]
    small[:, 1] = np.asarray(inp["k_norm_g"], f32)[0]
    small[0:8, 2] = np.asarray(inp["b_forget"], f32)[0]
    small[:, 8:16] = np.asarray(inp["pool_scale"], f32)[0].reshape(8, 128).T
    sh["small"] = small
    sh["ident8"] = np.ascontiguousarray(np.eye(128, 8, dtype=f32))
    sh["identf"] = np.ascontiguousarray(np.eye(128, dtype=f32))
    return sh


def _prep_core(inp, core):
    f32 = np.float32
    b, j = core // 4, core % 4
    x = np.asarray(inp["x"], f32)[b]
    xs = x.reshape(8, 4, 128, D)[:, j].reshape(T, D)
    d = {}
    d["xT"] = np.ascontiguousarray(xs.T.reshape(16, 128, T).transpose(1, 0, 2))
    cc = np.asarray(inp["c"], f32)
    d["cT"] = np.ascontiguousarray(cc.reshape(2, 16, 128).transpose(2, 1, 0).reshape(128, 32))
    wbv = np.zeros((128, 2), f32)
    wbv[:, b] = 1.0
    d["wb"] = wbv
    wsel = np.zeros((128, 4), f32)
    wsel[:, (j - 1) % 4] = 1.0
    d["wsel_halo"] = wsel
    mask = np.zeros((128, 4, 128), f32)
    for r in range(4):
        if r < j:
            mask[:, r, :] = 0.0
        elif r == j:
            s_ = np.arange(128)[:, None]
            t_ = np.arange(128)[None, :]
            mask[:, r, :] = np.where(s_ <= t_, 0.0, -30000.0)
        else:
            mask[:, r, :] = -30000.0
    d["maskc"] = mask.reshape(128, 512)
    invc = np.zeros((128, 4, 128), f32)
    for g in range(4):
        w = 2 ** (g + 1)
        if j == 0:
            invc[:, g, :] = 1.0 / np.minimum(np.arange(128) + 1, w)
        else:
            invc[:, g, :] = 1.0 / w
    d["invc"] = invc.reshape(128, 512)
    return d


def kernel(**inp):
    if "nc" not in _NC_CACHE:
        _NC_CACHE["nc"] = build_nc()
    nc = _NC_CACHE["nc"]
    sh = _prep_shared(inp)
    in_maps = []
    for core in range(8):
        d = _prep_core(inp, core)
        j = core % 4
        m = dict(sh)
        full = m.pop("wada")
        m["wada"] = full[18 * core:18 * core + 18]
        bfull = m.pop("badaT_full")
        m["badaT"] = np.ascontiguousarray(bfull[:, 18 * core:18 * core + 18])
        m["wb"] = d["wb"]
        m["xT"] = d["xT"]
        m["cT"] = d["cT"]
        m["maskc"] = d["maskc"]
        m["invc"] = d["invc"]
        m["wsel"] = d["wsel_halo"]
        ws2 = np.zeros((128, 4), np.float32)
        ws2[:, j] = 1.0
        m["wsel_own"] = ws2
        in_maps.append(m)
    res = run_bass_kernel_spmd(nc, in_maps, core_ids=list(range(8)))
    out = np.empty((2, 4096, D), np.float32)
    for core in range(8):
        b, j = core // 4, core % 4
        oT = res.results[core]["outT"]
        loc = oT.transpose(2, 1, 0).reshape(T, D)
        out[b].reshape(8, 4, 128, D)[:, j] = loc.reshape(8, 128, D)
    if DEBUG:
        kernel.last_dbg = [res.results[c]["dbg"] for c in range(8)]
    return out
```
